# Optimizing a Trainium2 kernel written in Bass

```python
import math
import jax, jax.numpy as jnp
from jax import lax
import numpy as np

D_MODEL = 2048
BATCH = 4
SEQ = 4096
DEPTH = 1

CHUNK = 64
N_META = 16
D_RWKV = 1024
RWKV_HEAD = 64
N_RWKV_HEADS = D_RWKV // RWKV_HEAD
D_DECAY_LORA = 64
D_AAA_LORA = 64
D_CONV = 1024
CONV_GROUP = 64
CONV_WIDTH = 3
RMS_EPS = 1e-6
LNX_EPS = 64e-5

RW_COLS = 4 * D_RWKV + D_DECAY_LORA + D_AAA_LORA
CV_COLS = 4 * D_CONV
IN_COLS = RW_COLS + CV_COLS
D_MIX = D_RWKV + D_CONV

kernel_name = "hymba_rwkv7_shortconv_block"


def _rmsnorm(x, g):
    xf = x.astype(jnp.float32)
    y = xf * lax.rsqrt(jnp.mean(xf * xf, axis=-1, keepdims=True) + RMS_EPS)
    return (y * g.astype(jnp.float32)).astype(x.dtype)


def _rwkv7_recurrence(r, decay, k, v, kk, b):
    bsz, L, H, N = r.shape
    xs = tuple(jnp.moveaxis(t.astype(jnp.float32), 1, 0) for t in (r, decay, k, v, kk, b))

    def step(S, inp):
        r_t, w_t, k_t, v_t, kk_t, b_t = inp
        sa = jnp.einsum('bhvk,bhk->bhv', S, kk_t)
        S = (S * w_t[:, :, None, :]
             - sa[..., None] * b_t[:, :, None, :]
             + v_t[..., None] * k_t[:, :, None, :])
        y_t = jnp.einsum('bhvk,bhk->bhv', S, r_t)
        return S, y_t

    S0 = jnp.zeros((bsz, H, N, N), jnp.float32)
    S, y_meta = lax.scan(step, S0, tuple(t[:N_META] for t in xs))
    real = tuple(t[N_META:].reshape((-1, CHUNK) + t.shape[1:]) for t in xs)

    def chunk_step(S, chunk):
        return lax.scan(step, S, chunk)

    S, y_real = lax.scan(chunk_step, S, real)
    y = jnp.concatenate([y_meta, y_real.reshape((-1,) + y_real.shape[2:])], axis=0)
    return jnp.moveaxis(y, 0, 1).astype(r.dtype)


def _causal_depthwise_conv(u, w):
    C = u.shape[-1]
    return lax.conv_general_dilated(
        u, w[:, None, :].astype(u.dtype),
        window_strides=(1,), padding=[(CONV_WIDTH - 1, 0)],
        dimension_numbers=('NWC', 'WIO', 'NWC'),
        feature_group_count=C)


def setup_inputs(seed: int = 0) -> dict:
    key = jax.random.key(seed)
    ks = jax.random.split(key, 20)
    f32 = jnp.float32
    x = jax.random.normal(ks[0], (BATCH, SEQ, D_MODEL), f32)
    meta_tokens = jax.random.normal(ks[1], (N_META, D_MODEL), f32)
    norm_in_g = 1.0 + 0.05 * jax.random.normal(ks[2], (DEPTH, D_MODEL), f32)
    w_in = jax.random.normal(ks[3], (DEPTH, D_MODEL, IN_COLS), f32) * D_MODEL ** -0.5
    mu_shift = jax.random.uniform(ks[4], (DEPTH, RW_COLS), f32)
    w0 = jax.random.uniform(ks[5], (DEPTH, D_RWKV), f32, minval=-6.0, maxval=1.0)
    w_lora_up = jax.random.normal(ks[6], (DEPTH, D_DECAY_LORA, D_RWKV), f32) * D_DECAY_LORA ** -0.5
    a0 = 0.1 * jax.random.normal(ks[7], (DEPTH, D_RWKV), f32)
    a_lora_up = jax.random.normal(ks[8], (DEPTH, D_AAA_LORA, D_RWKV), f32) * D_AAA_LORA ** -0.5
    k_k = 0.85 + 0.05 * jax.random.normal(ks[9], (DEPTH, D_RWKV), f32)
    k_a = 1.0 + 0.05 * jax.random.normal(ks[10], (DEPTH, D_RWKV), f32)
    r_k = 0.1 * jax.random.normal(ks[11], (DEPTH, N_RWKV_HEADS, RWKV_HEAD), f32)
    lnx_g = 1.0 + 0.05 * jax.random.normal(ks[12], (DEPTH, D_RWKV), f32)
    lnx_b = 0.01 * jax.random.normal(ks[13], (DEPTH, D_RWKV), f32)
    conv_w = jax.random.normal(ks[14], (DEPTH, CONV_WIDTH, D_CONV), f32) * CONV_WIDTH ** -0.5
    w_out = jax.random.normal(ks[15], (DEPTH, D_MIX, D_MODEL), f32) * D_MIX ** -0.5
    norm_f_g = 1.0 + 0.05 * jax.random.normal(ks[16], (D_MODEL,), f32)
    return {"x": x, "meta_tokens": meta_tokens, "norm_in_g": norm_in_g, "w_in": w_in,
            "mu_shift": mu_shift, "w0": w0, "w_lora_up": w_lora_up, "a0": a0,
            "a_lora_up": a_lora_up, "k_k": k_k, "k_a": k_a, "r_k": r_k,
            "lnx_g": lnx_g, "lnx_b": lnx_b, "conv_w": conv_w, "w_out": w_out,
            "norm_f_g": norm_f_g}


def reference(x, meta_tokens, norm_in_g, w_in, mu_shift, w0, w_lora_up, a0, a_lora_up,
              k_k, k_a, r_k, lnx_g, lnx_b, conv_w, w_out, norm_f_g):
    bsz = x.shape[0]
    meta = jnp.broadcast_to(meta_tokens[None].astype(x.dtype), (bsz, N_META, D_MODEL))
    h_res = jnp.concatenate([meta, x], axis=1)
    L = h_res.shape[1]
    H, N = N_RWKV_HEADS, RWKV_HEAD

    for l in range(DEPTH):
        hn = _rmsnorm(h_res, norm_in_g[l])
        P = hn @ w_in[l]
        Pr, Pc = P[..., :RW_COLS], P[..., RW_COLS:]

        prev = jnp.pad(Pr[:, :-1], ((0, 0), (1, 0), (0, 0)))
        Pr = Pr + (prev - Pr) * mu_shift[l]
        r = Pr[..., 0 * D_RWKV:1 * D_RWKV]
        k = Pr[..., 1 * D_RWKV:2 * D_RWKV]
        v = Pr[..., 2 * D_RWKV:3 * D_RWKV]
        g_r = Pr[..., 3 * D_RWKV:4 * D_RWKV]
        wl = Pr[..., 4 * D_RWKV:4 * D_RWKV + D_DECAY_LORA]
        al = Pr[..., 4 * D_RWKV + D_DECAY_LORA:]

        w_raw = -jax.nn.softplus(-(w0[l] + jnp.tanh(wl) @ w_lora_up[l])) - 0.5
        decay = jnp.exp(-jnp.exp(w_raw.astype(jnp.float32)))
        a = jax.nn.sigmoid(a0[l] + al @ a_lora_up[l])

        kk = (k * k_k[l]).reshape(bsz, L, H, N)
        kk = kk / jnp.maximum(jnp.linalg.norm(kk.astype(jnp.float32), axis=-1, keepdims=True), 1e-12).astype(kk.dtype)
        k = k * (1.0 + (a - 1.0) * k_a[l])

        rh = r.reshape(bsz, L, H, N)
        kh = k.reshape(bsz, L, H, N)
        vh = v.reshape(bsz, L, H, N)
        ah = a.reshape(bsz, L, H, N)
        y = _rwkv7_recurrence(rh, decay.reshape(bsz, L, H, N), kh, vh, kk, kk * ah)

        yf = y.astype(jnp.float32)
        mean = jnp.mean(yf, axis=-1, keepdims=True)
        var = jnp.mean(jnp.square(yf - mean), axis=-1, keepdims=True)
        yn = ((yf - mean) * lax.rsqrt(var + LNX_EPS)).reshape(bsz, L, D_RWKV)
        yn = (yn * lnx_g[l] + lnx_b[l]).astype(x.dtype)
        bonus = jnp.sum(rh * kh * r_k[l], axis=-1, keepdims=True) * vh
        y_a = (yn + bonus.reshape(bsz, L, D_RWKV)) * jax.nn.silu(g_r)

        Bg = Pc[..., 0 * D_CONV:1 * D_CONV]
        Cg = Pc[..., 1 * D_CONV:2 * D_CONV]
        hc = Pc[..., 2 * D_CONV:3 * D_CONV]
        g_c = Pc[..., 3 * D_CONV:4 * D_CONV]
        y_b = Bg * _causal_depthwise_conv(Cg * hc, conv_w[l]) * jax.nn.silu(g_c)

        mix = jnp.concatenate([y_a, y_b], axis=-1)
        h_res = h_res + mix @ w_out[l]

    out = _rmsnorm(h_res, norm_f_g)
    return out[:, N_META:]
```

```python
import numpy as np
import concourse.bass as bass
import concourse.mybir as mybir
from concourse.bass_utils import run_bass_kernel_spmd

F32 = mybir.dt.float32
BF16 = mybir.dt.bfloat16
ALU = mybir.AluOpType
AF = mybir.ActivationFunctionType

D = 2048
SEQ = 4096
NMETA = 16
TPB = 5
TB = TPB * 128
NBLK = 7
NT = NBLK * TPB
NZ = NT - 33
NCH = TB // 64
HALF = TB // 2
C0 = float(np.exp(-0.5))
NHP = 4
NCG = 4
NU = NHP + NCG
NG = 2 * NU
NCOL = NHP * 512 + 128 + NCG * 512
PV_HP = 16
NPV = 16 + NHP * 12 + 1 + NCG * 3
CS_M4, CS_MX, CS_ON, CS_SC = 0, 256, 320, 448
NCST = 448 + TB


class Prog:
    def __init__(self, nc, stack):
        self.nc = nc
        self.stack = stack
        self.eng = {"pe": nc.tensor, "act": nc.scalar, "dve": nc.vector, "pool": nc.gpsimd, "sp": nc.sync}
        self.cnt = {e: 0 for e in self.eng}
        self.sem = {}
        for e in ("pe", "act", "dve", "pool"):
            self.sem[e] = stack.enter_context(nc.semaphore("c_" + e))
        self.dcnt = {}
        self.res = {}
        self.waited = {e: {} for e in self.eng}
        self.ninstr = 0

    def _need(self, eng, reads, writes):
        need = {}

        def add(k, v):
            if v > need.get(k, 0):
                need[k] = v
        for r in reads:
            rec = self.res.get(r)
            if rec and rec["w"]:
                add(*rec["w"])
        for w in writes:
            rec = self.res.get(w)
            if rec:
                if rec["w"]:
                    add(*rec["w"])
                for k, v in rec["r"].items():
                    add(k, v)
        e = self.eng[eng]
        for k, v in need.items():
            if k == eng and eng in ("pe", "sp"):
                continue
            if k.startswith("d_"):
                v = self.dcnt[k]
            if v > self.waited[eng].get(k, 0):
                e.wait_ge(self.sem[k], v)
                self.waited[eng][k] = v
                self.ninstr += 1

    def _mark(self, key, val, reads, writes):
        for r in reads:
            rec = self.res.setdefault(r, {"w": None, "r": {}})
            rec["r"][key] = val
        for w in writes:
            self.res[w] = {"w": (key, val), "r": {}}

    def op(self, eng, fn, reads=(), writes=()):
        self._need(eng, reads, writes)
        ins = fn(self.eng[eng])
        self.cnt[eng] += 1
        ins.then_inc(self.sem[eng], 1)
        self.ninstr += 1
        self._mark(eng, self.cnt[eng], reads, writes)

    def dma(self, q, slot, out, in_, reads=(), writes=()):
        key = "d_" + slot
        if key not in self.sem:
            self.sem[key] = self.stack.enter_context(self.nc.semaphore(key))
            self.dcnt[key] = 0
        self._need(q, reads, writes)
        self.eng[q].dma_start(out=out, in_=in_).then_inc(self.sem[key], 16)
        self.dcnt[key] += 16
        self.ninstr += 1
        self._mark(key, self.dcnt[key], reads, writes)

    def collective(self, ins_ap, outs_ap, groups, reads=(), writes=()):
        key = "d_cc"
        if key not in self.sem:
            self.sem[key] = self.stack.enter_context(self.nc.semaphore(key))
            self.dcnt[key] = 0
        self._need("pool", reads, writes)
        self.nc.gpsimd.collective_compute("AllGather", ALU.bypass, replica_groups=groups, ins=[ins_ap],
                                          outs=[outs_ap]).then_inc(self.sem[key])
        self.dcnt[key] += 1
        self.ninstr += 1
        self._mark(key, self.dcnt[key], reads, writes)

    def finish(self, q="sp"):
        for key, v in self.dcnt.items():
            if v > self.waited[q].get(key, 0):
                self.eng[q].wait_ge(self.sem[key], v)


def build_nc():
    from contextlib import ExitStack
    nc = bass.Bass("TRN2", target_bir_lowering=False)
    x = nc.dram_tensor("x", [SEQ, D], F32, kind="ExternalInput").ap()
    meta = nc.dram_tensor("meta", [NMETA, D], F32, kind="ExternalInput").ap()
    w_in = nc.dram_tensor("w_in", [D, NCOL], F32, kind="ExternalInput").ap()
    pvec = nc.dram_tensor("pvec", [128, NPV], F32, kind="ExternalInput").ap()
    cst = nc.dram_tensor("cst", [128, NCST], F32, kind="ExternalInput").ap()
    ident_d = nc.dram_tensor("ident", [128, 128], F32, kind="ExternalInput").ap()
    lora = nc.dram_tensor("lora", [128, NHP * 128], F32, kind="ExternalInput").ap()
    w_out = nc.dram_tensor("w_out", [D, D], F32, kind="ExternalInput").ap()
    normf = nc.dram_tensor("normf", [128, D], F32, kind="ExternalInput").ap()
    xres = nc.dram_tensor("xres", [SEQ // 2, D], F32, kind="ExternalInput").ap()
    out = nc.dram_tensor("out", [SEQ // 2, D], F32, kind="ExternalOutput").ap()
    mixd = [nc.dram_tensor("mixd%d" % i, [NU, 128, TB], BF16, kind="Internal").ap() for i in range(NBLK)]
    gath = [nc.dram_tensor("gath%d" % i, [NG, 128, TB], BF16, kind="Internal").ap() for i in range(NBLK)]
    PAIRS = [[0, 1], [2, 3], [4, 5], [6, 7]]
    gathall = nc.dram_tensor("gathall", [128, NT * NG * 128], BF16, kind="Internal").ap()
    woutd = nc.dram_tensor("woutd", [D, D], BF16, kind="Internal").ap()
    NSL = NCOL // 128
    wind = nc.dram_tensor("wind", [NSL, 128, 16 * 128], BF16, kind="Internal").ap()

    with ExitStack() as st:
        P = Prog(nc, st)

        def sb(name, shape, dt=F32):
            return st.enter_context(nc.sbuf_tensor(name, shape, dt))

        def gc_copy(bk):
            for ti in range(TPB):
                P.dma("sp", "gc%d" % bk, gathall[:, (bk * TPB + ti) * NG * 128:(bk * TPB + ti + 1) * NG * 128].rearrange("p (u t) -> p u t", t=128), gath[bk][:, :, ti * 128:(ti + 1) * 128].rearrange("u p t -> p u t"),
                      reads=["gath%d" % bk], writes=["gathall%d_%d" % (bk, ti)])

        def wout_cast(u):
            P.dma("pool", "wc", woutd[u * 128:(u + 1) * 128, :], w_out[u * 128:(u + 1) * 128, :], reads=[], writes=["woutd%d" % u])

        def ps(name):
            return st.enter_context(nc.psum_tensor(name, [128, 512], F32))

        pv = sb("pv", [128, NPV])
        cs = sb("cs", [128, NCST])
        identb = sb("identb", [128, 128], BF16)
        loraW = sb("loraW", [128, NHP * 128], BF16)
        loraA = sb("loraA", [128, NHP * 128], BF16)
        rcarry = sb("rcarry", [128, NHP, 4])
        lcarry = sb("lcarry", [128, 1])
        ccarry = sb("ccarry", [128, NCG, 2])
        T32 = sb("T32", [128, NHP, 64])
        T0b = sb("T0b", [128, NHP, 2, 64], BF16)
        PSB = [ps("psb%d" % i) for i in range(8)]
        pj = [PSB[0], PSB[1]]
        ptr = PSB[2]
        pa, PAN = [PSB[3], PSB[6]], ["B3", "B6"]
        PI, PIN = [PSB[4], PSB[7]], ["B4", "B7"]
        PC, PCN = [PSB[5], PSB[2]], ["B5", "B2"]

        P.dma("sp", "c0", pv[:], pvec[:, :], writes=["pv"])
        P.dma("sp", "c1", cs[:], cst[:, :], writes=["cs"])
        P.dma("pool", "c4", identb[:], ident_d[:, :], writes=["identb"])
        for hp_ in range(NHP):
            c7 = 16 + 12 * hp_ + 7
            P.op("dve", lambda e, c7=c7: e.tensor_scalar(out=pv[:, c7 + 4:c7 + 5], in0=pv[:, c7:c7 + 1], scalar1=-1.0, scalar2=1.0,
                                                         op0=ALU.mult, op1=ALU.add), ["pv"], ["pv"])
        P.op("pool", lambda e: e.memset(rcarry[:], 0.0), [], ["rcarry"])
        P.op("pool", lambda e: e.memset(lcarry[:], 0.0), [], ["lcarry"])
        P.op("pool", lambda e: e.memset(ccarry[:], 0.0), [], ["ccarry"])
        P.op("pool", lambda e: e.memset(T32[:], 0.0), [], ["T32"])
        P.op("pool", lambda e: e.memset(T0b[:], 0.0), [], ["T0b"])
        P.op("pool", lambda e: e.memset(loraW[:], 0.0), [], ["loraW"])
        P.op("pool", lambda e: e.memset(loraA[:], 0.0), [], ["loraA"])

        mask4 = cs[:, CS_M4:CS_M4 + 256]
        maskx = cs[:, CS_MX:CS_MX + 64]
        onesbd = cs[:, CS_ON:CS_ON + 128]
        scanm = cs[:, CS_SC:CS_SC + TB]
        gin_b = pv[:, 0:16].unsqueeze(2).to_broadcast([128, 16, 128])

        with ExitStack() as s1:
            def sb1(name, shape, dt=F32):
                return s1.enter_context(nc.sbuf_tensor(name, shape, dt))
            lst = sb1("lst", [128, NHP * 128])
            xt = [sb1("xt%d" % i, [128, D]) for i in range(1)]
            xs = [sb1("xs%d" % i, [128, D], BF16) for i in range(2)]
            ssq = sb1("ssq", [128, 4])
            hnT = sb1("hnT", [128, 16, TB], BF16)
            wst = [sb1("wst%d" % i, [128, 16, 128]) for i in range(1)]
            wbf = [sb1("wbf%d" % i, [128, 4, 16, 128], BF16) for i in range(2)]
            Praw = [sb1("Praw%d" % i, [128, 4, TB + 1]) for i in range(2)]
            Plo = sb1("Plo", [128, TB + 1])
            lwin = sb1("lwin", [128, TB], BF16)
            tmpw = [sb1("tmp%d" % i, [128, TB + 2]) for i in range(9)]
            tmp = [t[:, 0:TB] for t in tmpw]
            KBH = sb1("KBH", [128, NCH, 2, 64], BF16)
            Vb = sb1("Vb", [128, TB], BF16)
            gam = [[sb1("gam%d_%d" % (b, i), [128, NCH]) for i in range(2)] for b in range(2)]
            QR = [[sb1("QR%d_%d" % (b, i), [128, NCH, 2, 64], BF16) for i in range(2)] for b in range(2)]
            KB = [[sb1("KB%d_%d" % (b, i), [128, NCH, 2, 64], BF16) for i in range(2)] for b in range(2)]
            KVt = [[sb1("KVt%d_%d" % (b, i), [128, NCH, 3, 64], BF16) for i in range(2)] for b in range(2)]
            bonus = [[sb1("bonus%d_%d" % (b, i), [128, TB]) for i in range(2)] for b in range(2)]
            sgate = [[sb1("sgate%d_%d" % (b, i), [128, TB]) for i in range(2)] for b in range(2)]
            ybuf = [sb1("ybuf%d" % i, [128, TB]) for i in range(2)]
            AM = [[sb1("AM%d_%d" % (i, j), [128, 4, 64], BF16) for j in range(3)] for i in range(2)]
            S = [[[sb1("S%d_%d_%d" % (i, j, k), [128, 3, 64], BF16) for k in range(2)] for j in range(2)] for i in range(2)]
            INV = [[sb1("INV%d_%d" % (i, j), [128, 64], BF16) for j in range(3)] for i in range(2)]
            SI = sb1("SI", [128, 64], BF16)
            RHSb = [sb1("RHSb%d" % i, [128, 64], BF16) for i in range(2)]
            Unb = [sb1("Unb%d" % i, [128, 64], BF16) for i in range(2)]
            mixT = [sb1("mixT%d" % i, [128, TB], BF16) for i in range(2)]
            Ubuf = tmpw[7]

            P.dma("sp", "c2", lst[:], lora[:, :], writes=["lst"])
            P.op("dve", lambda e: e.tensor_copy(out=loraW[0:64, :], in_=lst[0:64, :]), ["lst", "loraW"], ["loraW"])
            P.op("dve", lambda e: e.tensor_copy(out=loraA[64:128, :], in_=lst[64:128, :]), ["lst", "loraA"], ["loraA"])

            P.op("act", lambda e: e.copy(out=SI[0:64, :], in_=identb[0:64, 0:64]), ["identb"], ["SI"])
            P.op("act", lambda e: e.copy(out=SI[64:128, :], in_=identb[64:128, 64:128]), ["identb", "SI"], ["SI"])
            cur = {"blk": 0}
            wcount = [0]

            wq = []
            wq_slot = {}
            wq_next = [0]

            def unit_coffs(kind, i):
                if kind == "lora":
                    return [NHP * 512]
                if kind == "hp":
                    return [q * NHP * 128 + i * 128 for q in range(4)]
                cbase = NHP * 512 + 128
                return [cbase + q * NCG * 128 + i * 128 for q in range(4)]

            def build_wq():
                seq = [("lora", 0, 0), ("hp", 0, 0), ("hp", 1, 0)]
                for b in range(NBLK):
                    seq += [("hp", 2, b), ("hp", 3, b)] + [("cv", c, b) for c in range(NCG)]
                    if b + 1 < NBLK:
                        seq += [("lora", 0, b + 1), ("hp", 0, b + 1), ("hp", 1, b + 1)]
                for kind, i, b in seq:
                    wq.append((kind, i, b))
            build_wq()
            wq_index = {u: n for n, u in enumerate(wq)}

            def g_issue_upto(n):
                while wq_next[0] <= min(n, len(wq) - 1):
                    idx = wq_next[0]
                    wq_next[0] += 1
                    kind, i, blk = wq[idx]
                    us = idx % 2
                    wq_slot[idx] = us
                    for q, co in enumerate(unit_coffs(kind, i)):
                        sidx = co // 128
                        wdst = wbf[us][:, q, :, :]
                        if blk == 0:
                            P.dma("sp", "w0", wst[0][:], w_in[:, co:co + 128].rearrange("(kc p) c -> p kc c", p=128),
                                  writes=["wst0"])
                            P.op("pool", lambda e, wdst=wdst: e.tensor_tensor(out=wdst, in0=wst[0][:], in1=gin_b, op=ALU.mult),
                                 ["wst0", "pv"], ["wbf%d_%d" % (us, q)])
                            P.dma("pool", "wsv%d_%d" % (us, q), wind[sidx, :, :], wdst.rearrange("p a b -> p (a b)"),
                                  reads=["wbf%d_%d" % (us, q)], writes=["wind%d" % sidx])
                        else:
                            P.dma("sp", "wd%d_%d" % (us, q), wdst.rearrange("p a b -> p (a b)"), wind[sidx, :, :],
                                  reads=["wind%d" % sidx], writes=["wbf%d_%d" % (us, q)])
                        yield

            def g_load_weights(unit, hold):
                idx = wq_index[unit]
                yield from g_issue_upto(idx)
                hold[0] = wq_slot[idx]
                yield from g_issue_upto(idx + 1)

            def run(gen):
                for _ in gen:
                    pass

            def pump(gen, n):
                if gen is None:
                    return
                for _ in range(n):
                    try:
                        next(gen)
                    except StopIteration:
                        return

            def merge(ga, gb):
                alive = [ga, gb]
                while alive:
                    for g in list(alive):
                        try:
                            next(g)
                            yield
                        except StopIteration:
                            alive.remove(g)

            def chain(*gens):
                for g in gens:
                    yield from g


            pjc = [0]

            pend_ev = [None]

            def g_inproj(us, q, evac):
                for hf in range(2):
                    b = pjc[0] % 2
                    pjc[0] += 1
                    off = hf * HALF
                    for kc in range(16):
                        P.op("pe", lambda e, kc=kc, b=b, off=off: e.matmul(pj[b][:, 0:HALF], lhsT=wbf[us][:, q, kc, :],
                                                                          rhs=hnT[:, kc, off:off + HALF],
                                                                          start=(kc == 0), stop=(kc == 15)),
                             ["wbf%d_%d" % (us, q), "hnT"], ["B%d" % b])
                        if kc % 4 == 3:
                            yield
                    if pend_ev[0] is not None:
                        pend_ev[0]()
                    pend_ev[0] = (lambda evac=evac, b=b, off=off: evac(pj[b][:, 0:HALF], off, "B%d" % b))
                    yield

            def flush_ev():
                if pend_ev[0] is not None:
                    pend_ev[0]()
                    pend_ev[0] = None

            def inproj(us, q, evac):
                run(g_inproj(us, q, evac))
                flush_ev()

            xcnt = [0]
            def g_rwkv_inproj(hp, blk):
                pb = 16 + 12 * hp

                def pcol(i, pb=pb):
                    return pv[:, pb + i:pb + i + 1]
                hold = [0]
                yield from g_load_weights(("hp", hp, blk), hold)
                us = hold[0]
                pr = Praw[hp % 2]
                prq = ["Praw%d_%d" % (hp % 2, q) for q in range(4)]
                P.op("act", lambda e: e.copy(out=pr[:, :, 0], in_=rcarry[:, hp, :]), ["rcarry"], prq)
                for q in range(4):
                    def ev(psap, off, pres, q=q):
                        P.op("act", lambda e: e.copy(out=pr[:, q, 1 + off:1 + off + HALF], in_=psap), [pres], [prq[q]])
                    yield from g_inproj(us, q, ev)
                flush_ev()
                P.op("act", lambda e: e.copy(out=rcarry[:, hp, :], in_=pr[:, :, TB]), prq, ["rcarry"])

            pend_tr = [None]

            def g_flush_tr():
                if pend_tr[0] is not None:
                    t = pend_tr[0]
                    pend_tr[0] = None
                    yield from t()

            def g_prep(hp, hl, bs):
                pb = 16 + 12 * hp

                def pcol(i, pb=pb):
                    return pv[:, pb + i:pb + i + 1]
                pr = Praw[hp % 2]
                prq = ["Praw%d_%d" % (hp % 2, q) for q in range(4)]
                QRh, KBh, KVth, gamh = QR[bs][hl], KB[bs][hl], KVt[bs][hl], gam[bs][hl]
                qrn, kbn, kvn, gmn = "QR%d_%d" % (bs, hl), "KB%d_%d" % (bs, hl), "KVt%d_%d" % (bs, hl), "gam%d_%d" % (bs, hl)
                bon, bonn = bonus[bs][hl], "bonus%d_%d" % (bs, hl)
                r_, k_, v_, g_ = pr[:, 0, 0:TB], pr[:, 1, 0:TB], pr[:, 2, 0:TB], pr[:, 3, 0:TB]
                sg, a_, kk, tk, k2, b_, csg, csx = tmp[1], tmp[2], tmp[3], tmp[4], tmp[5], tmp[6], tmp[7], tmp[4]
                e1, e2, e3 = tmp[0], tmp[4], tmp[8]
                kt, bt = tmp[1], tmp[3]

                def v3(t):
                    return t[:].rearrange("p (c s) -> p c s", s=64)

                def v3a(ap):
                    return ap.rearrange("p (c s) -> p c s", s=64)

                def shift(q):
                    P.op("dve", lambda e: e.tensor_sub(out=tmp[8][:], in0=pr[:, q, 0:TB], in1=pr[:, q, 1:TB + 1]), [prq[q]], ["tmp8"])
                    P.op("dve", lambda e: e.scalar_tensor_tensor(out=pr[:, q, 0:TB], in0=tmp[8][:], scalar=pcol(q),
                                                                 in1=pr[:, q, 1:TB + 1], op0=ALU.mult, op1=ALU.add),
                         ["tmp8", prq[q], "pv"], [prq[q]])

                def lora_mm(hf):
                    off = hf * HALF
                    P.op("pe", lambda e: e.matmul(pj[0][:, 0:HALF], lhsT=loraW[:, hp * 128:(hp + 1) * 128],
                                                  rhs=lwin[:, off:off + HALF], start=True, stop=True), ["loraW", "lwin"], ["B0"])
                    P.op("pe", lambda e: e.matmul(pj[1][:, 0:HALF], lhsT=loraA[:, hp * 128:(hp + 1) * 128],
                                                  rhs=lwin[:, off:off + HALF], start=True, stop=True), ["loraA", "lwin"], ["B1"])

                def lora_ev(hf):
                    off = hf * HALF
                    P.op("act", lambda e: e.activation(out=sg[:, off:off + HALF], in_=pj[0][:, 0:HALF], func=AF.Sigmoid,
                                                       bias=pcol(4)), ["B0", "pv"], ["tmp1"])
                    P.op("act", lambda e: e.activation(out=a_[:, off:off + HALF], in_=pj[1][:, 0:HALF], func=AF.Sigmoid,
                                                       bias=pcol(5)), ["B1", "pv"], ["tmp2"])

                def stat_mm(src, srcn):
                    for hf in range(2):
                        off = hf * HALF
                        P.op("pe", lambda e, off=off, hf=hf: e.matmul(pj[hf][:, 0:HALF], lhsT=onesbd, rhs=src[:, off:off + HALF],
                                                                      start=True, stop=True), ["cs", srcn], ["B%d" % hf])

                lora_mm(0)
                yield
                yield
                shift(1)
                yield
                yield
                P.op("act", lambda e: e.activation(out=kk[:], in_=k_, func=AF.Identity, scale=pcol(6)), [prq[1], "pv"], ["tmp3"])
                P.op("act", lambda e: e.activation(out=tmp[0][:], in_=k_, func=AF.Square, scale=pcol(6)), [prq[1], "pv"], ["tmp0"])
                yield
                yield
                lora_ev(0)
                yield
                yield
                shift(0)
                yield
                yield
                shift(2)
                yield
                yield
                yield from g_flush_tr()
                lora_mm(1)
                yield
                yield
                shift(3)
                yield
                yield
                lora_ev(1)
                yield
                yield
                P.op("act", lambda e: e.activation(out=sgate[bs][hl][:], in_=g_, func=AF.Silu), [prq[3]], ["sgate%d_%d" % (bs, hl)])
                yield
                yield
                P.op("dve", lambda e: e.tensor_tensor_scan(out=csg[:], data0=scanm, data1=sg[:], initial=0.0, op0=ALU.mult,
                                                           op1=ALU.add), ["cs", "tmp1"], ["tmp7"])
                yield
                yield
                stat_mm(tmp[0], "tmp0")
                yield
                yield
                P.op("act", lambda e: e.activation(out=tk[:], in_=a_[:], func=AF.Identity, scale=pcol(7), bias=pv[:, pb + 11:pb + 12]),
                     ["tmp2", "pv"], ["tmp4"])
                P.op("dve", lambda e: e.tensor_mul(out=k2[:], in0=k_, in1=tk[:]), [prq[1], "tmp4"], ["tmp5"])
                P.op("dve", lambda e: e.scalar_tensor_tensor(out=tmp[6][:], in0=r_, scalar=pcol(8), in1=k2[:], op0=ALU.mult,
                                                             op1=ALU.mult), [prq[0], "pv", "tmp5"], ["tmp6"])
                P.op("dve", lambda e: e.tensor_sub(out=csx[:], in0=csg[:], in1=sg[:]), ["tmp7", "tmp1", "tmp5"], ["tmp4"])
                yield
                yield
                P.op("act", lambda e: e.activation(out=e3[:], in_=csg[:], func=AF.Exp, scale=C0), ["tmp7"], ["tmp8"])
                P.op("act", lambda e: e.activation(out=gamh[:], in_=csg[:].rearrange("p (c s) -> p c s", s=64)[:, :, 63],
                                                   func=AF.Exp, scale=-C0), ["tmp7"], [gmn])
                yield
                yield
                P.op("dve", lambda e: e.tensor_mul(out=kt[:], in0=k2[:], in1=e3[:]), ["tmp5", "tmp8", "tmp1", "tmp7"], ["tmp1"])
                yield
                yield
                for hf in range(2):
                    off = hf * HALF
                    P.op("dve", lambda e, off=off, hf=hf: e.tensor_scalar(out=tmp[0][:, off:off + HALF], in0=pj[hf][:, 0:HALF],
                                                                          scalar1=1e-24, scalar2=None, op0=ALU.max),
                         ["B%d" % hf], ["tmp0"])
                yield
                yield
                stat_mm(tmp[6], "tmp6")
                yield
                yield
                P.op("act", lambda e: e.activation(out=tmp[0][:], in_=tmp[0][:], func=AF.Sqrt), ["tmp0"], ["tmp0"])
                yield
                yield
                P.op("act", lambda e: e.activation(out=e2[:], in_=csx[:], func=AF.Exp, scale=-C0), ["tmp4"], ["tmp4"])
                yield
                yield
                P.op("dve", lambda e: e.tensor_tensor(out=KBH[:, :, 0, :], in0=v3(kt), in1=gamh[:, :].unsqueeze(2).to_broadcast([128, NCH, 64]),
                                                      op=ALU.mult), ["tmp1", gmn], ["KBH"])
                yield
                yield
                P.op("dve", lambda e: e.reciprocal(out=tmp[0][:], in_=tmp[0][:]), ["tmp0"], ["tmp0"])
                P.op("dve", lambda e: e.tensor_mul(out=kk[:], in0=kk[:], in1=tmp[0][:]), ["tmp3", "tmp0"], ["tmp3"])
                yield
                yield
                P.op("dve", lambda e: e.tensor_mul(out=b_[:], in0=kk[:], in1=a_[:]), ["tmp3", "tmp2"], ["tmp6"])
                yield
                yield
                for hf in range(2):
                    off = hf * HALF
                    P.op("dve", lambda e, off=off, hf=hf: e.tensor_mul(out=bon[:, off:off + HALF], in0=pj[hf][:, 0:HALF],
                                                                       in1=pr[:, 2, off:off + HALF]), ["B%d" % hf, prq[2]], [bonn])
                yield
                yield
                P.op("act", lambda e: e.copy(out=KBh[:, :, 0, :], in_=v3(kt)), ["tmp1"], [kbn])
                P.op("act", lambda e: e.copy(out=Vb[:], in_=v_), [prq[2]], ["Vb"])
                yield
                yield
                P.op("dve", lambda e: e.tensor_mul(out=QRh[:, :, 0, :], in0=v3(kk), in1=v3(e2)), ["tmp3", "tmp4"], [qrn])
                yield
                yield
                P.op("dve", lambda e: e.tensor_mul(out=bt[:], in0=b_[:], in1=e3[:]), ["tmp6", "tmp8", qrn], ["tmp3"])
                yield
                yield
                e1 = tmp[6]
                P.op("act", lambda e: e.activation(out=e1[:], in_=csg[:], func=AF.Exp, scale=-C0), ["tmp7", "tmp3"], ["tmp6"])
                yield
                yield
                gam_b = gamh[:, :].unsqueeze(2).to_broadcast([128, NCH, 64])
                P.op("act", lambda e: e.copy(out=KBh[:, :, 1, :], in_=v3(bt)), ["tmp3", kbn], [kbn])
                P.op("dve", lambda e: e.tensor_tensor(out=KBH[:, :, 1, :], in0=v3(bt), in1=gam_b, op=ALU.mult),
                     ["tmp3", gmn, "KBH"], ["KBH"])
                yield
                yield
                P.op("dve", lambda e: e.tensor_mul(out=QRh[:, :, 1, :], in0=v3a(r_), in1=v3(e1)), [prq[0], "tmp6", qrn], [qrn])
                yield
                yield
                def tr():
                    for c2 in range(0, NCH, 2):
                        for cc in range(2):
                            c = c2 + cc
                            for h in range(2):
                                ph = slice(64 * h, 64 * h + 64)
                                for m in range(3):
                                    src = KBH[ph, c, m, :] if m < 2 else Vb[ph, c * 64:(c + 1) * 64]
                                    P.op("pe", lambda e, src=src, ph=ph, h=h, cc=cc, m=m: e.matmul(
                                        pj[c2 // 2 % 2][ph, cc * 192 + m * 64:cc * 192 + m * 64 + 64], lhsT=src,
                                        rhs=identb[ph, 64 * h:64 * h + 64], start=True, stop=True, tile_position=(64 * h, 64 * h)),
                                        ["KBH", "Vb", "identb"], ["B%d" % (c2 // 2 % 2)])
                            yield
                            yield
                        if c2 >= 2:
                            pc2 = c2 - 2
                            P.op("act", lambda e, pc2=pc2: e.copy(out=KVth[:, pc2:pc2 + 2, :, :].rearrange("p a b c -> p (a b c)"),
                                                                  in_=pj[pc2 // 2 % 2][:, 0:384]), ["B%d" % (pc2 // 2 % 2)], [kvn])
                    pc2 = NCH - 2
                    P.op("act", lambda e: e.copy(out=KVth[:, pc2:pc2 + 2, :, :].rearrange("p a b c -> p (a b c)"),
                                                 in_=pj[pc2 // 2 % 2][:, 0:384]), ["B%d" % (pc2 // 2 % 2)], [kvn])
                pend_tr[0] = tr

            def stage2(grp, bs, filler, rate, part):
                def heads():
                    for h in range(2):
                        yield slice(64 * h, 64 * h + 64), (64 * h, 64 * h), h

                def stA(c, hl):
                    am = AM[hl][c % 3]
                    amn = "AM%d_%d" % (hl, c % 3)
                    pah, pan = pa[hl], PAN[hl]
                    QRh, KBh = QR[bs][hl], KB[bs][hl]
                    qrn, kbn = "QR%d_%d" % (bs, hl), "KB%d_%d" % (bs, hl)
                    for ph, tp, h in heads():
                        P.op("pe", lambda e, ph=ph, tp=tp: e.matmul(pah[ph, 0:128], lhsT=KBh[ph, c, 0, :],
                                                                    rhs=QRh[ph, c, :, :].rearrange("p a b -> p (a b)"),
                                                                    start=True, stop=True, tile_position=tp), [kbn, qrn], [pan])
                        P.op("pe", lambda e, ph=ph, tp=tp: e.matmul(pah[ph, 128:256], lhsT=KBh[ph, c, 1, :],
                                                                    rhs=QRh[ph, c, :, :].rearrange("p a b -> p (a b)"),
                                                                    start=True, stop=True, tile_position=tp), [kbn, qrn], [pan])
                        P.op("pe", lambda e, ph=ph, tp=tp: e.matmul(pah[ph, 256:320], lhsT=QRh[ph, c, 0, :], rhs=KBh[ph, c, 1, :],
                                                                    start=True, stop=True, tile_position=tp), [kbn, qrn], [pan])
                    P.op("dve", lambda e: e.tensor_tensor(out=am[:].rearrange("p a b -> p (a b)"), in0=pah[:, 0:256], in1=mask4,
                                                          op=ALU.mult), [pan, "cs"], [amn])
                    s0 = S[hl][c % 2][0]
                    s0n = "S%d_%d_0" % (hl, c % 2)
                    P.op("act", lambda e: e.copy(out=s0[:, 0, :], in_=am[:, 2, :]), [amn], [s0n])
                    P.op("dve", lambda e: e.tensor_tensor(out=s0[:, 2, :], in0=pah[:, 256:320], in1=maskx, op=ALU.mult),
                         [pan, "cs", s0n], [s0n])

                def stInv(c, hl, kstep):
                    si, so = S[hl][c % 2][kstep % 2], S[hl][c % 2][(kstep + 1) % 2]
                    sin, son = "S%d_%d_%d" % (hl, c % 2, kstep % 2), "S%d_%d_%d" % (hl, c % 2, (kstep + 1) % 2)
                    last = kstep == 5
                    pih = pa[hl][:, 320:512] if kstep < 2 else PI[hl][:, 0:192]
                    pin = PAN[hl] if kstep < 2 else PIN[hl]
                    pk = SI if kstep == 0 else si[:, 1, :]
                    for ph, tp, h in heads():
                        pkh = SI[ph, :] if kstep == 0 else si[ph, 1, :]
                        if kstep == 0 or last:
                            if not last:
                                P.op("pe", lambda e, ph=ph, tp=tp: e.matmul(pih[ph, 0:64], lhsT=si[ph, 2, :], rhs=si[ph, 0, :],
                                                                            start=True, stop=True, tile_position=tp), [sin], [pin])
                            P.op("pe", lambda e, ph=ph, tp=tp, pkh=pkh: e.matmul(pih[ph, 64:128], lhsT=si[ph, 2, :], rhs=pkh,
                                                                                 start=True, stop=False, tile_position=tp),
                                 [sin, "SI"], [pin])
                        else:
                            P.op("pe", lambda e, ph=ph, tp=tp: e.matmul(pih[ph, 0:128], lhsT=si[ph, 2, :],
                                                                        rhs=si[ph, 0:2, :].rearrange("p a b -> p (a b)"),
                                                                        start=True, stop=False, tile_position=tp), [sin], [pin])
                        P.op("pe", lambda e, ph=ph, tp=tp, pkh=pkh, h=h: e.matmul(pih[ph, 64:128], lhsT=identb[ph, 64 * h:64 * h + 64],
                                                                                  rhs=pkh, start=False, stop=True, tile_position=tp),
                             [sin, "SI", "identb"], [pin])
                        if not last:
                            P.op("pe", lambda e, ph=ph, tp=tp: e.matmul(pih[ph, 128:192], lhsT=si[ph, 0, :], rhs=si[ph, 2, :],
                                                                        start=True, stop=True, tile_position=tp), [sin], [pin])
                    if not last:
                        P.op("act", lambda e: e.copy(out=so[:].rearrange("p a b -> p (a b)"), in_=pih[:, 0:192]), [pin], [son])
                    else:
                        P.op("act", lambda e: e.copy(out=INV[hl][c % 3][:], in_=pih[:, 64:128]), [pin], ["INV%d_%d" % (hl, c % 3)])

                def stRHS(c, hl):
                    hp = grp * 2 + hl
                    am, amn = AM[hl][c % 3], "AM%d_%d" % (hl, c % 3)
                    par = c % 2
                    t0, t0n = T0b[:, hp, par, :], "T0b%d_%d" % (hp, par)
                    pch = PC[hl][:, 0:192]
                    for ph, tp, h in heads():
                        P.op("pe", lambda e, ph=ph, tp=tp: e.matmul(pch[ph, 0:64], lhsT=am[ph, 0, :], rhs=KVt[bs][hl][ph, c, 2, :],
                                                                    start=True, stop=False, tile_position=tp),
                             [amn, "KVt%d_%d" % (bs, hl)], [PCN[hl]])
                        P.op("pe", lambda e, ph=ph, tp=tp: e.matmul(pch[ph, 0:64], lhsT=QR[bs][hl][ph, c, 0, :], rhs=t0[ph, :],
                                                                    start=False, stop=True, tile_position=tp),
                             ["QR%d_%d" % (bs, hl), t0n, "T0b"], [PCN[hl]])
                    P.op("act", lambda e: e.copy(out=RHSb[hl][:], in_=pch[:, 0:64]), [PCN[hl]], ["RHSb%d" % hl])

                def stU(c, hl):
                    pch = PC[hl][:, 0:192]
                    inv, invn = INV[hl][c % 3], "INV%d_%d" % (hl, c % 3)
                    for ph, tp, h in heads():
                        P.op("pe", lambda e, ph=ph, tp=tp: e.matmul(pch[ph, 64:128], lhsT=inv[ph, :], rhs=RHSb[hl][ph, :],
                                                                    start=True, stop=True, tile_position=tp),
                             [invn, "RHSb%d" % hl], [PCN[hl]])
                    P.op("act", lambda e: e.mul(out=Unb[hl][:], in_=pch[:, 64:128], mul=-1.0),
                         [PCN[hl]], ["Unb%d" % hl])

                def stTY(c, hl):
                    hp = grp * 2 + hl
                    am, amn = AM[hl][c % 3], "AM%d_%d" % (hl, c % 3)
                    par = c % 2
                    t0, t0n = T0b[:, hp, par, :], "T0b%d_%d" % (hp, par)
                    t1, t1n = T0b[:, hp, 1 - par, :], "T0b%d_%d" % (hp, 1 - par)
                    pch = PC[hl][:, 0:192]
                    pyh = PC[hl][:, 192:256]
                    pyn = PCN[hl]
                    kvn, unn = "KVt%d_%d" % (bs, hl), "Unb%d" % hl
                    for ph, tp, h in heads():
                        P.op("pe", lambda e, ph=ph, tp=tp: e.matmul(pyh[ph, :], lhsT=t0[ph, :], rhs=QR[bs][hl][ph, c, 1, :],
                                                                    start=True, stop=False, tile_position=tp),
                             [t0n, "QR%d_%d" % (bs, hl), "T0b"], [pyn])
                        P.op("pe", lambda e, ph=ph, tp=tp: e.matmul(pyh[ph, :], lhsT=KVt[bs][hl][ph, c, 2, :], rhs=am[ph, 1, :],
                                                                    start=False, stop=False, tile_position=tp), [kvn, amn], [pyn])
                        P.op("pe", lambda e, ph=ph, tp=tp: e.matmul(pyh[ph, :], lhsT=Unb[hl][ph, :], rhs=am[ph, 3, :],
                                                                    start=False, stop=True, tile_position=tp), [unn, amn], [pyn])
                    for ph, tp, h in heads():
                        P.op("pe", lambda e, ph=ph, tp=tp: e.matmul(pch[ph, 128:192], lhsT=KVt[bs][hl][ph, c, 0, :], rhs=KVt[bs][hl][ph, c, 2, :],
                                                                    start=True, stop=False, tile_position=tp), [kvn], [PCN[hl]])
                        P.op("pe", lambda e, ph=ph, tp=tp: e.matmul(pch[ph, 128:192], lhsT=KVt[bs][hl][ph, c, 1, :], rhs=Unb[hl][ph, :],
                                                                    start=False, stop=True, tile_position=tp), [kvn, unn], [PCN[hl]])
                    P.op("dve", lambda e: e.scalar_tensor_tensor(out=T32[:, hp, :], in0=T32[:, hp, :], scalar=gam[bs][hl][:, c:c + 1],
                                                                 in1=pch[:, 128:192], op0=ALU.mult, op1=ALU.add),
                         ["T32_%d" % hp, "T32", "gam%d_%d" % (bs, hl), PCN[hl]], ["T32_%d" % hp])
                    P.op("act", lambda e: e.copy(out=t1, in_=T32[:, hp, :]), ["T32_%d" % hp, "T0b"], [t1n])
                    P.op("act", lambda e: e.copy(out=ybuf[hl][:, c * 64:(c + 1) * 64], in_=pyh), [pyn], ["ybuf%d" % hl])

                def inv_steps(c, ks):
                    for k in ks:
                        for hl in range(2):
                            stInv(c, hl, k)
                        pump(filler, rate)

                if part == "prologue":
                    def gen():
                        for hl in range(2):
                            stA(0, hl)
                            yield
                        for k in range(6):
                            for hl in range(2):
                                stInv(0, hl, k)
                                yield
                        if NCH > 1:
                            for hl in range(2):
                                stA(1, hl)
                                yield
                            for k in (0, 1):
                                for hl in range(2):
                                    stInv(1, hl, k)
                                    yield
                    return gen()
                for c in range(NCH):
                    n1, n2 = c + 1 < NCH, c + 2 < NCH
                    if n2:
                        for hl in range(2):
                            stA(c + 2, hl)
                        pump(filler, rate)
                    for hl in range(2):
                        stRHS(c, hl)
                    pump(filler, rate)
                    if n1:
                        inv_steps(c + 1, (2,))
                    if n2:
                        inv_steps(c + 2, (0,))
                    for hl in range(2):
                        stU(c, hl)
                    pump(filler, rate)
                    if n1:
                        inv_steps(c + 1, (3,))
                    if n2:
                        inv_steps(c + 2, (1,))
                    if n1:
                        inv_steps(c + 1, (4,))
                    for hl in range(2):
                        stTY(c, hl)
                    pump(filler, rate)
                    if n1:
                        inv_steps(c + 1, (5,))
                if filler is not None:
                    run(filler)

            def stage3(grp, bs, blk, as_gen=False):
                def one(hl):
                    hp = grp * 2 + hl
                    pb = 16 + 12 * hp
                    y_ = ybuf[hl]
                    yn_ = "ybuf%d" % hl
                    ti = (4, 5, 6) if hl == 0 else (1, 2, 3)
                    ysq, m_, r2 = tmp[ti[0]], tmp[ti[1]], tmp[ti[2]]
                    nsq, nm, nr = "tmp%d" % ti[0], "tmp%d" % ti[1], "tmp%d" % ti[2]
                    pbk = (pj[0], pj[1]) if hl == 0 else (PC[0], PC[1])
                    pbn = ("B0", "B1") if hl == 0 else (PCN[0], PCN[1])
                    P.op("dve", lambda e: e.tensor_mul(out=ysq[:], in0=y_[:], in1=y_[:]), [yn_], [nsq])
                    yield
                    for hf in range(2):
                        off = hf * HALF
                        P.op("pe", lambda e, off=off: e.matmul(pbk[0][:, 0:HALF], lhsT=onesbd, rhs=y_[:, off:off + HALF],
                                                               start=True, stop=True), ["cs", yn_], [pbn[0]])
                        P.op("pe", lambda e, off=off: e.matmul(pbk[1][:, 0:HALF], lhsT=onesbd, rhs=ysq[:, off:off + HALF],
                                                               start=True, stop=True), ["cs", nsq], [pbn[1]])
                        yield
                        P.op("dve", lambda e, off=off: e.tensor_scalar(out=m_[:, off:off + HALF], in0=pbk[0][:, 0:HALF], scalar1=1.0 / 64,
                                                                       scalar2=None, op0=ALU.mult), [pbn[0]], [nm])
                        P.op("dve", lambda e, off=off: e.tensor_mul(out=r2[:, off:off + HALF], in0=m_[:, off:off + HALF],
                                                                    in1=m_[:, off:off + HALF]), [nm], [nr])
                        P.op("dve", lambda e, off=off: e.scalar_tensor_tensor(out=r2[:, off:off + HALF], in0=pbk[1][:, 0:HALF], scalar=1.0 / 64,
                                                                              in1=r2[:, off:off + HALF], op0=ALU.mult, op1=ALU.subtract),
                             [pbn[1], nr], [nr])
                        yield
                    P.op("dve", lambda e: e.tensor_scalar(out=r2[:], in0=r2[:], scalar1=64e-5, scalar2=None, op0=ALU.add), [nr], [nr])
                    yield
                    P.op("act", lambda e: e.activation(out=r2[:], in_=r2[:], func=AF.Sqrt), [nr], [nr])
                    yield
                    P.op("dve", lambda e: e.tensor_sub(out=y_[:], in0=y_[:], in1=m_[:]), [yn_, nm], [yn_])
                    yield
                    P.op("dve", lambda e: e.reciprocal(out=ysq[:], in_=r2[:]), [nr, nsq], [nsq])
                    P.op("dve", lambda e: e.tensor_mul(out=y_[:], in0=y_[:], in1=ysq[:]), [yn_, nsq], [yn_])
                    yield
                    P.op("dve", lambda e: e.tensor_scalar(out=y_[:], in0=y_[:], scalar1=pv[:, pb + 9:pb + 10],
                                                          scalar2=pv[:, pb + 10:pb + 11], op0=ALU.mult, op1=ALU.add), [yn_, "pv"], [yn_])
                    P.op("dve", lambda e: e.tensor_add(out=y_[:], in0=y_[:], in1=bonus[bs][hl][:]), [yn_, "bonus%d_%d" % (bs, hl)], [yn_])
                    yield
                    mx = mixT[hp % 2]
                    mxn = "mixT%d" % (hp % 2)
                    P.op("dve", lambda e: e.tensor_mul(out=mx[:], in0=y_[:], in1=sgate[bs][hl][:]), [yn_, "sgate%d_%d" % (bs, hl)], [mxn])
                    P.dma("pool", "m%d" % (hp % 2), mixd[blk][hp, :, :], mx[:], reads=[mxn], writes=["mixd%d_%d" % (blk, hp)])
                g3 = merge(one(0), one(1))
                if as_gen:
                    return g3
                run(g3)

            def g_conv_unit(cg, blk):
                cb = 16 + 12 * NHP + 1 + 3 * cg
                cbase = NHP * 512 + 128
                hold = [0]
                yield from g_load_weights(("cv", cg, blk), hold)
                us = hold[0]
                pr = Praw[cg % 2]
                prq = ["Praw%d_%d" % (cg % 2, q) for q in range(4)]
                for q in range(4):
                    def ev(psap, off, pres, q=q):
                        P.op("act", lambda e: e.copy(out=pr[:, q, 1 + off:1 + off + HALF], in_=psap), [pres], [prq[q]])
                    yield from g_inproj(us, q, ev)
                flush_ev()
                yield
                P.op("act", lambda e: e.copy(out=Ubuf[:, 0:2], in_=ccarry[:, cg, :]), ["ccarry"], ["tmp7"])
                P.op("dve", lambda e: e.tensor_mul(out=Ubuf[:, 2:TB + 2], in0=pr[:, 1, 1:TB + 1], in1=pr[:, 2, 1:TB + 1]),
                     [prq[1], prq[2], "tmp7"], ["tmp7"])
                P.op("act", lambda e: e.copy(out=ccarry[:, cg, :], in_=Ubuf[:, TB:TB + 2]), ["tmp7"], ["ccarry"])
                cv = tmp[0]
                yield
                P.op("dve", lambda e: e.tensor_scalar(out=cv[:], in0=Ubuf[:, 0:TB], scalar1=pv[:, cb:cb + 1], scalar2=None,
                                                      op0=ALU.mult), ["tmp7", "pv"], ["tmp0"])
                P.op("dve", lambda e: e.scalar_tensor_tensor(out=cv[:], in0=Ubuf[:, 1:TB + 1], scalar=pv[:, cb + 1:cb + 2], in1=cv[:],
                                                             op0=ALU.mult, op1=ALU.add), ["tmp7", "pv", "tmp0"], ["tmp0"])
                P.op("dve", lambda e: e.scalar_tensor_tensor(out=cv[:], in0=Ubuf[:, 2:TB + 2], scalar=pv[:, cb + 2:cb + 3], in1=cv[:],
                                                             op0=ALU.mult, op1=ALU.add), ["tmp7", "pv", "tmp0"], ["tmp0"])
                P.op("dve", lambda e: e.tensor_mul(out=cv[:], in0=cv[:], in1=pr[:, 0, 1:TB + 1]), ["tmp0", prq[0]], ["tmp0"])
                yield
                P.op("act", lambda e: e.activation(out=tmp[5][:], in_=pr[:, 3, 1:TB + 1], func=AF.Silu), [prq[3]], ["tmp5"])
                u = NHP + cg
                mx = mixT[u % 2]
                mxn = "mixT%d" % (u % 2)
                P.op("dve", lambda e: e.tensor_mul(out=mx[:], in0=cv[:], in1=tmp[5][:]), ["tmp0", "tmp5"], [mxn])
                P.dma("pool", "m%d" % (u % 2), mixd[blk][u, :, :], mx[:], reads=[mxn], writes=["mixd%d_%d" % (blk, u)])


            def g_xphase(blk):
                pend = [None]

                def transposes(sl, tcols):
                    for g4 in range(4):
                        for j in range(4):
                            kc = g4 * 4 + j
                            P.op("pe", lambda e, kc=kc, j=j: e.matmul(pj[1][:, j * 128:(j + 1) * 128],
                                                                      lhsT=xs[sl][:, kc * 128:(kc + 1) * 128],
                                                                      rhs=identb[:, :], start=True, stop=True),
                                 ["xs%d" % sl, "identb"], ["B1"])
                        yield
                        if g4 % 2 == 0:
                            P.op("act", lambda e, g4=g4: e.copy(out=hnT[:, g4 * 4:g4 * 4 + 4, tcols],
                                                                in_=pj[1][:, :].rearrange("p (a b) -> p a b", b=128)),
                                 ["B1"], ["hnT"])
                        else:
                            P.op("dve", lambda e, g4=g4: e.tensor_copy(out=hnT[:, g4 * 4:g4 * 4 + 4, tcols],
                                                                       in_=pj[1][:, :].rearrange("p (a b) -> p a b", b=128)),
                                 ["B1"], ["hnT"])
                        yield
                for ti in range(TPB):
                    gt = blk * TPB + ti
                    tcols = slice(ti * 128, (ti + 1) * 128)
                    if gt < NZ:
                        P.op("pool", lambda e, tcols=tcols: e.memset(hnT[:, :, tcols], 0.0), [], ["hnT"])
                        continue
                    sl = xcnt[0] % 2
                    xcnt[0] += 1
                    if gt == NZ:
                        P.op("pool", lambda e: e.memset(xt[0][:], 0.0), [], ["xt0"])
                        P.dma("sp", "x0", xt[0][112:128, :], meta[:, :], reads=["xt0"], writes=["xt0"])
                    else:
                        r0 = (gt - NZ - 1) * 128
                        P.dma("sp", "x0", xt[0][:], x[r0:r0 + 128, :], writes=["xt0"])
                    yield
                    P.op("pool", lambda e: e.memset(ssq[:, 0:1], 0.0), [], ["ssq"])
                    P.op("act", lambda e, sl=sl: e.activation(out=xs[sl][:], in_=xt[0][:], func=AF.Square,
                                                              accum_out=ssq[:, 0:1]), ["xt0", "ssq"], ["xs%d" % sl, "ssq", "lst"])
                    yield
                    P.op("dve", lambda e: e.tensor_scalar(out=ssq[:, 2:3], in0=ssq[:, 0:1], scalar1=1.0 / D, scalar2=1e-6,
                                                          op0=ALU.mult, op1=ALU.add), ["ssq"], ["ssq"])
                    yield
                    P.op("act", lambda e: e.activation(out=ssq[:, 1:2], in_=ssq[:, 2:3], func=AF.Sqrt), ["ssq"], ["ssq"])
                    yield
                    P.op("dve", lambda e: e.reciprocal(out=ssq[:, 3:4], in_=ssq[:, 1:2]), ["ssq"], ["ssq"])
                    P.op("dve", lambda e, sl=sl: e.tensor_scalar(out=xs[sl][:], in0=xt[0][:], scalar1=ssq[:, 3:4], scalar2=None,
                                                                 op0=ALU.mult), ["xt0", "ssq"], ["xs%d" % sl])
                    yield
                    if pend[0] is not None:
                        yield from transposes(*pend[0])
                    pend[0] = (sl, tcols)
                if pend[0] is not None:
                    yield from transposes(*pend[0])


            def g_lora(blk):
                hold = [0]
                yield from g_load_weights(("lora", 0, blk), hold)
                us = hold[0]
                yield
                P.op("act", lambda e: e.copy(out=Plo[:, 0:1], in_=lcarry[:, 0:1]), ["lcarry"], ["Plo"])

                def ev_lora(psap, off, pres):
                    P.op("act", lambda e: e.copy(out=Plo[:, 1 + off:1 + off + HALF], in_=psap), [pres], ["Plo"])
                yield from g_inproj(us, 0, ev_lora)
                flush_ev()
                yield
                P.op("act", lambda e: e.copy(out=lcarry[:, 0:1], in_=Plo[:, TB:TB + 1]), ["Plo"], ["lcarry"])
                mucol = pv[:, 16 + 12 * NHP:16 + 12 * NHP + 1]
                yield
                P.op("dve", lambda e: e.tensor_sub(out=tmp[0][:], in0=Plo[:, 0:TB], in1=Plo[:, 1:TB + 1]), ["Plo"], ["tmp0"])
                yield
                P.op("dve", lambda e: e.scalar_tensor_tensor(out=tmp[1][:], in0=tmp[0][:], scalar=mucol, in1=Plo[:, 1:TB + 1],
                                                             op0=ALU.mult, op1=ALU.add), ["tmp0", "Plo", "pv"], ["tmp1"])
                yield
                P.op("act", lambda e: e.activation(out=lwin[0:64, :], in_=tmp[1][0:64, :], func=AF.Tanh), ["tmp1"], ["lwin"])
                yield
                P.op("dve", lambda e: e.tensor_copy(out=lwin[64:128, :], in_=tmp[1][64:128, :]), ["tmp1", "lwin"], ["lwin"])


            def block_tail(blk):
                if blk > 1:
                    gc_copy(blk - 2)
                if 1 <= blk <= 4:
                    for u in range((blk - 1) * 4, blk * 4):
                        wout_cast(u)
                if blk > 0:
                    P.collective(mixd[blk - 1].rearrange("u p t -> (u p) t"), gath[blk - 1].rearrange("u p t -> (u p) t"), PAIRS,
                                 reads=["mixd%d_%d" % (blk - 1, u) for u in range(NU)], writes=["gath%d" % (blk - 1)])

            run(g_xphase(0))
            run(g_lora(0))
            run(g_rwkv_inproj(0, 0))
            run(g_rwkv_inproj(1, 0))
            run(g_prep(0, 0, 0))
            run(g_prep(1, 1, 0))
            run(g_flush_tr())
            run(stage2(0, 0, None, 0, "prologue"))
            for blk in range(NBLK):
                block_tail(blk)
                f0 = chain(g_rwkv_inproj(2, blk), g_rwkv_inproj(3, blk), g_prep(2, 0, 1), g_prep(3, 1, 1), g_conv_unit(0, blk), g_flush_tr())
                stage2(0, 0, f0, 2, "main")
                run(merge(stage2(1, 1, None, 0, "prologue"), stage3(0, 0, blk, as_gen=True)))
                gens = [g_conv_unit(cg, blk) for cg in range(1, NCG)]
                if blk + 1 < NBLK:
                    gens += [g_xphase(blk + 1), g_lora(blk + 1), g_rwkv_inproj(0, blk + 1), g_rwkv_inproj(1, blk + 1),
                             g_prep(0, 0, 0), g_prep(1, 1, 0), g_flush_tr()]
                stage2(1, 1, chain(*gens), 3, "main")
                if blk + 1 < NBLK:
                    run(merge(stage2(0, 0, None, 0, "prologue"), stage3(1, 1, blk, as_gen=True)))
                else:
                    stage3(1, 1, blk)

        with ExitStack() as s2:
            def sb2(name, shape, dt=F32):
                return s2.enter_context(nc.sbuf_tensor(name, shape, dt))
            woutb = sb2("woutb", [128, NG, D], BF16)
            nf = sb2("nf", [128, D])
            mixl = [sb2("mixl%d" % i, [128, NG, 128], BF16) for i in range(2)]
            xr = [sb2("xr%d" % i, [128, D]) for i in range(2)]
            hh = [sb2("hh%d" % i, [128, D]) for i in range(2)]
            oo = [sb2("oo%d" % i, [128, D]) for i in range(2)]
            junk2 = sb2("junk2", [128, D], BF16)
            sq2 = [sb2("sq2_%d" % i, [128, 16]) for i in range(2)]
            bar = sb2("bar", [128, 16])
            dram_pref = ("gath", "mixd", "wind", "woutd")
            allres = [r for r in P.res.keys() if not r.startswith(dram_pref)]
            P.collective(mixd[NBLK - 1].rearrange("u p t -> (u p) t"), gath[NBLK - 1].rearrange("u p t -> (u p) t"), PAIRS,
                         reads=["mixd%d_%d" % (NBLK - 1, u) for u in range(NU)], writes=["gath%d" % (NBLK - 1)])
            P.dma("sp", "c3", bar[:, 0:8], normf[0:128, 0:8], reads=allres, writes=["nf"] + allres)
            def wout_chunk(oc):
                for u in range(NG):
                    P.dma("sp", "wo%d" % oc, woutb[:, u, oc * 512:(oc + 1) * 512], woutd[u * 128:(u + 1) * 128, oc * 512:(oc + 1) * 512],
                          reads=["woutd%d" % u, "nf"], writes=["woutb%d" % oc] if u == NG - 1 else [])
            wout_chunk(0)
            par2 = nc.sync.partition_id() % 2
            NT2 = 16
            pbank = [PSB[0], PSB[1], PSB[3], PSB[4]]
            pbn = ["B0", "B1", "B3", "B4"]
            pjc2 = 0
            pend = None
            for i in range(NT2):
                sl = i % 2
                if i == 6:
                    gc_copy(NBLK - 1)
                tile0 = par2 * NT2 + i
                gtmax = NZ + 1 + NT2 + i
                P.dma("sp", "ml%d" % sl, mixl[sl][:], gathall[:, bass.ds(par2 * (NT2 * NG * 128) + (NZ + 1 + i) * NG * 128, NG * 128)].rearrange("p (u t) -> p u t", t=128),
                      reads=["nf", "gathall%d_%d" % (gtmax // TPB, gtmax % TPB), "gathall%d_%d" % ((gtmax - NT2) // TPB, (gtmax - NT2) % TPB)],
                      writes=["mixl%d" % sl])
                P.dma("sp", "xr%d" % sl, xr[sl][:], xres[i * 128:(i + 1) * 128, :], reads=["nf"], writes=["xr%d" % sl])
                if i == 0:
                    for oc in range(1, 4):
                        wout_chunk(oc)
                    P.dma("sp", "c5", nf[:], normf[:, :], reads=["nf"], writes=["nf"])
                    gc_copy(NBLK - 2)
                hs = hh[sl]
                for oc in range(4):
                    b = pjc2 % 4
                    pjc2 += 1
                    for u in range(NG):
                        P.op("pe", lambda e, u=u, sl=sl, oc=oc, b=b: e.matmul(pbank[b][:, :], lhsT=mixl[sl][:, u, :],
                                                                             rhs=woutb[:, u, oc * 512:(oc + 1) * 512],
                                                                             start=(u == 0), stop=(u == NG - 1)),
                             ["mixl%d" % sl, "woutb%d" % oc], [pbn[b]])
                    P.op("dve", lambda e, sl=sl, oc=oc, b=b, hs=hs: e.tensor_tensor(out=hs[:, oc * 512:(oc + 1) * 512], in0=pbank[b][:, :],
                                                                                   in1=xr[sl][:, oc * 512:(oc + 1) * 512], op=ALU.add),
                         [pbn[b], "xr%d" % sl], ["hh%d" % sl])
                sq = sq2[sl][:, 0:4]
                sqn = "sq2_%d" % sl
                P.op("pool", lambda e, sq=sq: e.memset(sq[:, 0:1], 0.0), [], [sqn])
                P.op("act", lambda e, sq=sq, hs=hs: e.activation(out=junk2[:], in_=hs[:], func=AF.Square, accum_out=sq[:, 0:1]),
                     ["hh%d" % sl, sqn], ["junk2", sqn])
                if pend is not None:
                    pend()
                P.op("dve", lambda e, sq=sq: e.tensor_scalar(out=sq[:, 1:2], in0=sq[:, 0:1], scalar1=1.0 / D, scalar2=1e-6, op0=ALU.mult,
                                                             op1=ALU.add), [sqn], [sqn])
                P.op("act", lambda e, sq=sq: e.activation(out=sq[:, 2:3], in_=sq[:, 1:2], func=AF.Sqrt), [sqn], [sqn])

                def fin(i=i, sl=sl, sq=sq, hs=hs, sqn=sqn):
                    P.op("dve", lambda e: e.reciprocal(out=sq[:, 3:4], in_=sq[:, 2:3]), [sqn], [sqn])
                    P.op("dve", lambda e: e.scalar_tensor_tensor(out=oo[sl][:], in0=hs[:], scalar=sq[:, 3:4], in1=nf[:],
                                                                 op0=ALU.mult, op1=ALU.mult), ["hh%d" % sl, sqn, "nf"], ["oo%d" % sl])
                    P.dma("pool", "o%d" % sl, out[i * 128:(i + 1) * 128, :], oo[sl][:], reads=["oo%d" % sl], writes=[])
                pend = fin
            pend()
            P.finish("sp")
            P.finish("pool")
    return nc


def _consts():
    c = np.zeros((128, NCST), np.float32)
    s = np.arange(64)[:, None]
    t = np.arange(64)[None, :]
    strict = (s < t).astype(np.float32)
    incl = (s <= t).astype(np.float32)
    m4 = np.concatenate([strict, incl, -strict, incl], axis=1)
    c[0:64, CS_M4:CS_M4 + 256] = m4
    c[64:128, CS_M4:CS_M4 + 256] = m4
    mx = -(s > t).astype(np.float32)
    c[0:64, CS_MX:CS_MX + 64] = mx
    c[64:128, CS_MX:CS_MX + 64] = mx
    c[0:64, CS_ON:CS_ON + 64] = 1.0
    c[64:128, CS_ON + 64:CS_ON + 128] = 1.0
    sc = np.ones((TB,), np.float32)
    sc[::64] = 0.0
    c[:, CS_SC:CS_SC + TB] = sc[None, :]
    return c


def _colT(v):
    return np.ascontiguousarray(np.asarray(v, np.float32).reshape(-1, 128).T)


def kernel(x, meta_tokens, norm_in_g, w_in, mu_shift, w0, w_lora_up, a0, a_lora_up, k_k, k_a, r_k,
           lnx_g, lnx_b, conv_w, w_out, norm_f_g):
    x = np.asarray(x, np.float32)
    w_in0 = np.asarray(w_in, np.float32)[0]
    mu = np.asarray(mu_shift, np.float32)[0]
    cols = {
        0: _colT(mu[0:1024]), 1: _colT(mu[1024:2048]), 2: _colT(mu[2048:3072]), 3: _colT(mu[3072:4096]),
        4: _colT(np.asarray(w0)[0]), 5: _colT(np.asarray(a0)[0]), 6: _colT(np.asarray(k_k)[0]), 7: _colT(np.asarray(k_a)[0]),
        8: _colT(np.asarray(r_k)[0].reshape(-1)), 9: _colT(np.asarray(lnx_g)[0]), 10: _colT(np.asarray(lnx_b)[0]),
    }
    cw = np.asarray(conv_w, np.float32)[0]
    gin = _colT(np.asarray(norm_in_g, np.float32)[0])
    lora_full = np.concatenate([np.asarray(w_lora_up, np.float32)[0], np.asarray(a_lora_up, np.float32)[0]], axis=0)
    cst = _consts()
    ident = np.eye(128, dtype=np.float32)
    wout0 = np.asarray(w_out, np.float32)[0]
    rows = []
    for r in range(2):
        for u in range(NHP):
            rows.append(np.arange((r * NHP + u) * 128, (r * NHP + u + 1) * 128))
        for u in range(NCG):
            rows.append(1024 + np.arange((r * NCG + u) * 128, (r * NCG + u + 1) * 128))
    wout_r = np.ascontiguousarray(wout0[np.concatenate(rows)])
    normf = np.ascontiguousarray(np.broadcast_to(np.asarray(norm_f_g, np.float32)[None, :], (128, D)))
    meta = np.ascontiguousarray(np.asarray(meta_tokens, np.float32))
    shards = []
    for half in range(2):
        pvec = np.zeros((128, NPV), np.float32)
        pvec[:, 0:16] = gin
        for hp in range(NHP):
            for i, arr in cols.items():
                pvec[:, 16 + 12 * hp + i] = arr[:, half * NHP + hp]
        pvec[:, 16 + 12 * NHP] = mu[4096:4224]
        for cg in range(NCG):
            for j in range(3):
                ch0 = (half * NCG + cg) * 128
                pvec[:, 16 + 12 * NHP + 1 + 3 * cg + j] = cw[j, ch0:ch0 + 128]
        csel = []
        for q in range(4):
            csel.append(np.arange(q * 1024 + half * NHP * 128, q * 1024 + (half + 1) * NHP * 128))
        csel.append(np.arange(4096, 4224))
        for q in range(4):
            csel.append(4224 + np.arange(q * 1024 + half * NCG * 128, q * 1024 + (half + 1) * NCG * 128))
        w_sh = np.ascontiguousarray(w_in0[:, np.concatenate(csel)])
        lora = np.ascontiguousarray(lora_full[:, half * NHP * 128:(half + 1) * NHP * 128])
        shards.append((pvec, w_sh, lora))
    nc = build_nc()
    in_maps = []
    for c in range(8):
        b, half = c // 2, c % 2
        pvec, w_sh, lora = shards[half]
        in_maps.append({"x": np.ascontiguousarray(x[b]), "xres": np.ascontiguousarray(x[b, half * (SEQ // 2):(half + 1) * (SEQ // 2)]), "meta": meta, "w_in": w_sh, "pvec": pvec,
                        "cst": cst, "ident": ident, "lora": lora, "w_out": wout_r, "normf": normf})
    res = run_bass_kernel_spmd(nc, in_maps, core_ids=list(range(8)))
    return np.stack([np.concatenate([np.asarray(res.results[2 * b]["out"], np.float32),
                                     np.asarray(res.results[2 * b + 1]["out"], np.float32)], axis=0) for b in range(4)], axis=0)
```

```python
import numpy as np
import concourse.bass as bass
import concourse.mybir as mybir
from concourse.bass_utils import run_bass_kernel_spmd

F32 = mybir.dt.float32
BF16 = mybir.dt.bfloat16
ALU = mybir.AluOpType
AF = mybir.ActivationFunctionType

D = 2048
SEQ = 4096
NMETA = 16
TPB = 5
TB = TPB * 128
NBLK = 7
NT = NBLK * TPB
NZ = NT - 33
NCH = TB // 64
HALF = TB // 2
C0 = float(np.exp(-0.5))
NHP = 4
NCG = 4
NU = NHP + NCG
NG = 2 * NU
NCOL = NHP * 512 + 128 + NCG * 512
PV_HP = 16
NPV = 16 + NHP * 12 + 1 + NCG * 3
CS_M4, CS_MX, CS_ON, CS_SC = 0, 256, 320, 448
NCST = 448 + TB


class Prog:
    def __init__(self, nc, stack):
        self.nc = nc
        self.stack = stack
        self.eng = {"pe": nc.tensor, "act": nc.scalar, "dve": nc.vector, "pool": nc.gpsimd, "sp": nc.sync}
        self.cnt = {e: 0 for e in self.eng}
        self.sem = {}
        for e in ("pe", "act", "dve", "pool"):
            self.sem[e] = stack.enter_context(nc.semaphore("c_" + e))
        self.dcnt = {}
        self.res = {}
        self.waited = {e: {} for e in self.eng}
        self.ninstr = 0

    def _need(self, eng, reads, writes):
        need = {}

        def add(k, v):
            if v > need.get(k, 0):
                need[k] = v
        for r in reads:
            rec = self.res.get(r)
            if rec and rec["w"]:
                add(*rec["w"])
        for w in writes:
            rec = self.res.get(w)
            if rec:
                if rec["w"]:
                    add(*rec["w"])
                for k, v in rec["r"].items():
                    add(k, v)
        e = self.eng[eng]
        for k, v in need.items():
            if k == eng and eng in ("pe", "sp"):
                continue
            if k.startswith("d_"):
                v = self.dcnt[k]
            if v > self.waited[eng].get(k, 0):
                e.wait_ge(self.sem[k], v)
                self.waited[eng][k] = v
                self.ninstr += 1

    def _mark(self, key, val, reads, writes):
        for r in reads:
            rec = self.res.setdefault(r, {"w": None, "r": {}})
            rec["r"][key] = val
        for w in writes:
            self.res[w] = {"w": (key, val), "r": {}}

    def op(self, eng, fn, reads=(), writes=()):
        self._need(eng, reads, writes)
        ins = fn(self.eng[eng])
        self.cnt[eng] += 1
        ins.then_inc(self.sem[eng], 1)
        self.ninstr += 1
        self._mark(eng, self.cnt[eng], reads, writes)

    def dma(self, q, slot, out, in_, reads=(), writes=()):
        key = "d_" + slot
        if key not in self.sem:
            self.sem[key] = self.stack.enter_context(self.nc.semaphore(key))
            self.dcnt[key] = 0
        self._need(q, reads, writes)
        self.eng[q].dma_start(out=out, in_=in_).then_inc(self.sem[key], 16)
        self.dcnt[key] += 16
        self.ninstr += 1
        self._mark(key, self.dcnt[key], reads, writes)

    def collective(self, ins_ap, outs_ap, groups, reads=(), writes=()):
        key = "d_cc"
        if key not in self.sem:
            self.sem[key] = self.stack.enter_context(self.nc.semaphore(key))
            self.dcnt[key] = 0
        self._need("pool", reads, writes)
        self.nc.gpsimd.collective_compute("AllGather", ALU.bypass, replica_groups=groups, ins=[ins_ap],
                                          outs=[outs_ap]).then_inc(self.sem[key])
        self.dcnt[key] += 1
        self.ninstr += 1
        self._mark(key, self.dcnt[key], reads, writes)

    def finish(self, q="sp"):
        for key, v in self.dcnt.items():
            if v > self.waited[q].get(key, 0):
                self.eng[q].wait_ge(self.sem[key], v)


def build_nc():
    from contextlib import ExitStack
    nc = bass.Bass("TRN2", target_bir_lowering=False)
    x = nc.dram_tensor("x", [SEQ, D], F32, kind="ExternalInput").ap()
    meta = nc.dram_tensor("meta", [NMETA, D], F32, kind="ExternalInput").ap()
    w_in = nc.dram_tensor("w_in", [D, NCOL], F32, kind="ExternalInput").ap()
    pvec = nc.dram_tensor("pvec", [128, NPV], F32, kind="ExternalInput").ap()
    cst = nc.dram_tensor("cst", [128, NCST], F32, kind="ExternalInput").ap()
    ident_d = nc.dram_tensor("ident", [128, 128], F32, kind="ExternalInput").ap()
    lora = nc.dram_tensor("lora", [128, NHP * 128], F32, kind="ExternalInput").ap()
    w_out = nc.dram_tensor("w_out", [D, D], F32, kind="ExternalInput").ap()
    normf = nc.dram_tensor("normf", [128, D], F32, kind="ExternalInput").ap()
    xres = nc.dram_tensor("xres", [SEQ // 2, D], F32, kind="ExternalInput").ap()
    out = nc.dram_tensor("out", [SEQ // 2, D], F32, kind="ExternalOutput").ap()
    mixd = [nc.dram_tensor("mixd%d" % i, [NU, 128, TB], BF16, kind="Internal").ap() for i in range(NBLK)]
    gath = [nc.dram_tensor("gath%d" % i, [NG, 128, TB], BF16, kind="Internal").ap() for i in range(NBLK)]
    PAIRS = [[0, 1], [2, 3], [4, 5], [6, 7]]
    gathall = nc.dram_tensor("gathall", [128, NT * NG * 128], BF16, kind="Internal").ap()
    woutd = nc.dram_tensor("woutd", [D, D], BF16, kind="Internal").ap()
    NSL = NCOL // 128
    wind = nc.dram_tensor("wind", [NSL, 128, 16 * 128], BF16, kind="Internal").ap()

    with ExitStack() as st:
        P = Prog(nc, st)

        def sb(name, shape, dt=F32):
            return st.enter_context(nc.sbuf_tensor(name, shape, dt))

        def gc_copy(bk):
            for ti in range(TPB):
                P.dma("sp", "gc%d" % bk, gathall[:, (bk * TPB + ti) * NG * 128:(bk * TPB + ti + 1) * NG * 128].rearrange("p (u t) -> p u t", t=128), gath[bk][:, :, ti * 128:(ti + 1) * 128].rearrange("u p t -> p u t"),
                      reads=["gath%d" % bk], writes=["gathall%d_%d" % (bk, ti)])

        def wout_cast(u):
            P.dma("pool", "wc", woutd[u * 128:(u + 1) * 128, :], w_out[u * 128:(u + 1) * 128, :], reads=[], writes=["woutd%d" % u])

        def ps(name):
            return st.enter_context(nc.psum_tensor(name, [128, 512], F32))

        pv = sb("pv", [128, NPV])
        cs = sb("cs", [128, NCST])
        identb = sb("identb", [128, 128], BF16)
        loraW = sb("loraW", [128, NHP * 128], BF16)
        loraA = sb("loraA", [128, NHP * 128], BF16)
        rcarry = sb("rcarry", [128, NHP, 4])
        lcarry = sb("lcarry", [128, 1])
        ccarry = sb("ccarry", [128, NCG, 2])
        T32 = sb("T32", [128, NHP, 64])
        T0b = sb("T0b", [128, NHP, 2, 64], BF16)
        PSB = [ps("psb%d" % i) for i in range(8)]
        pj = [PSB[0], PSB[1]]
        ptr = PSB[2]
        pa, PAN = [PSB[3], PSB[6]], ["B3", "B6"]
        PI, PIN = [PSB[4], PSB[7]], ["B4", "B7"]
        PC, PCN = [PSB[5], PSB[2]], ["B5", "B2"]

        P.dma("sp", "c0", pv[:], pvec[:, :], writes=["pv"])
        P.dma("sp", "c1", cs[:], cst[:, :], writes=["cs"])
        P.dma("pool", "c4", identb[:], ident_d[:, :], writes=["identb"])
        for hp_ in range(NHP):
            c7 = 16 + 12 * hp_ + 7
            P.op("dve", lambda e, c7=c7: e.tensor_scalar(out=pv[:, c7 + 4:c7 + 5], in0=pv[:, c7:c7 + 1], scalar1=-1.0, scalar2=1.0,
                                                         op0=ALU.mult, op1=ALU.add), ["pv"], ["pv"])
        P.op("pool", lambda e: e.memset(rcarry[:], 0.0), [], ["rcarry"])
        P.op("pool", lambda e: e.memset(lcarry[:], 0.0), [], ["lcarry"])
        P.op("pool", lambda e: e.memset(ccarry[:], 0.0), [], ["ccarry"])
        P.op("pool", lambda e: e.memset(T32[:], 0.0), [], ["T32"])
        P.op("pool", lambda e: e.memset(T0b[:], 0.0), [], ["T0b"])
        P.op("pool", lambda e: e.memset(loraW[:], 0.0), [], ["loraW"])
        P.op("pool", lambda e: e.memset(loraA[:], 0.0), [], ["loraA"])

        mask4 = cs[:, CS_M4:CS_M4 + 256]
        maskx = cs[:, CS_MX:CS_MX + 64]
        onesbd = cs[:, CS_ON:CS_ON + 128]
        scanm = cs[:, CS_SC:CS_SC + TB]
        gin_b = pv[:, 0:16].unsqueeze(2).to_broadcast([128, 16, 128])

        with ExitStack() as s1:
            def sb1(name, shape, dt=F32):
                return s1.enter_context(nc.sbuf_tensor(name, shape, dt))
            lst = sb1("lst", [128, NHP * 128])
            xt = [sb1("xt%d" % i, [128, D]) for i in range(1)]
            xs = [sb1("xs%d" % i, [128, D], BF16) for i in range(2)]
            ssq = sb1("ssq", [128, 4])
            hnT = sb1("hnT", [128, 16, TB], BF16)
            wst = [sb1("wst%d" % i, [128, 16, 128]) for i in range(1)]
            wbf = [sb1("wbf%d" % i, [128, 4, 16, 128], BF16) for i in range(2)]
            Praw = [sb1("Praw%d" % i, [128, 4, TB + 1]) for i in range(2)]
            Plo = sb1("Plo", [128, TB + 1])
            lwin = sb1("lwin", [128, TB], BF16)
            tmpw = [sb1("tmp%d" % i, [128, TB + 2]) for i in range(9)]
            tmp = [t[:, 0:TB] for t in tmpw]
            KBH = sb1("KBH", [128, NCH, 2, 64], BF16)
            Vb = sb1("Vb", [128, TB], BF16)
            gam = [[sb1("gam%d_%d" % (b, i), [128, NCH]) for i in range(2)] for b in range(2)]
            QR = [[sb1("QR%d_%d" % (b, i), [128, NCH, 2, 64], BF16) for i in range(2)] for b in range(2)]
            KB = [[sb1("KB%d_%d" % (b, i), [128, NCH, 2, 64], BF16) for i in range(2)] for b in range(2)]
            KVt = [[sb1("KVt%d_%d" % (b, i), [128, NCH, 3, 64], BF16) for i in range(2)] for b in range(2)]
            bonus = [[sb1("bonus%d_%d" % (b, i), [128, TB]) for i in range(2)] for b in range(2)]
            sgate = [[sb1("sgate%d_%d" % (b, i), [128, TB]) for i in range(2)] for b in range(2)]
            ybuf = [sb1("ybuf%d" % i, [128, TB]) for i in range(2)]
            AM = [[sb1("AM%d_%d" % (i, j), [128, 4, 64], BF16) for j in range(3)] for i in range(2)]
            S = [[[sb1("S%d_%d_%d" % (i, j, k), [128, 3, 64], BF16) for k in range(2)] for j in range(2)] for i in range(2)]
            INV = [[sb1("INV%d_%d" % (i, j), [128, 64], BF16) for j in range(3)] for i in range(2)]
            SI = sb1("SI", [128, 64], BF16)
            RHSb = [sb1("RHSb%d" % i, [128, 64], BF16) for i in range(2)]
            Unb = [sb1("Unb%d" % i, [128, 64], BF16) for i in range(2)]
            mixT = [sb1("mixT%d" % i, [128, TB], BF16) for i in range(2)]
            Ubuf = tmpw[7]

            P.dma("sp", "c2", lst[:], lora[:, :], writes=["lst"])
            P.op("dve", lambda e: e.tensor_copy(out=loraW[0:64, :], in_=lst[0:64, :]), ["lst", "loraW"], ["loraW"])
            P.op("dve", lambda e: e.tensor_copy(out=loraA[64:128, :], in_=lst[64:128, :]), ["lst", "loraA"], ["loraA"])

            P.op("act", lambda e: e.copy(out=SI[0:64, :], in_=identb[0:64, 0:64]), ["identb"], ["SI"])
            P.op("act", lambda e: e.copy(out=SI[64:128, :], in_=identb[64:128, 64:128]), ["identb", "SI"], ["SI"])
            cur = {"blk": 0}
            wcount = [0]

            wq = []
            wq_slot = {}
            wq_next = [0]

            def unit_coffs(kind, i):
                if kind == "lora":
                    return [NHP * 512]
                if kind == "hp":
                    return [q * NHP * 128 + i * 128 for q in range(4)]
                cbase = NHP * 512 + 128
                return [cbase + q * NCG * 128 + i * 128 for q in range(4)]

            def build_wq():
                seq = [("lora", 0, 0), ("hp", 0, 0), ("hp", 1, 0)]
                for b in range(NBLK):
                    seq += [("hp", 2, b), ("hp", 3, b)] + [("cv", c, b) for c in range(NCG)]
                    if b + 1 < NBLK:
                        seq += [("lora", 0, b + 1), ("hp", 0, b + 1), ("hp", 1, b + 1)]
                for kind, i, b in seq:
                    wq.append((kind, i, b))
            build_wq()
            wq_index = {u: n for n, u in enumerate(wq)}

            def g_issue_upto(n):
                while wq_next[0] <= min(n, len(wq) - 1):
                    idx = wq_next[0]
                    wq_next[0] += 1
                    kind, i, blk = wq[idx]
                    us = idx % 2
                    wq_slot[idx] = us
                    for q, co in enumerate(unit_coffs(kind, i)):
                        sidx = co // 128
                        wdst = wbf[us][:, q, :, :]
                        if blk == 0:
                            P.dma("sp", "w0", wst[0][:], w_in[:, co:co + 128].rearrange("(kc p) c -> p kc c", p=128),
                                  writes=["wst0"])
                            P.op("pool", lambda e, wdst=wdst: e.tensor_tensor(out=wdst, in0=wst[0][:], in1=gin_b, op=ALU.mult),
                                 ["wst0", "pv"], ["wbf%d_%d" % (us, q)])
                            P.dma("pool", "wsv%d_%d" % (us, q), wind[sidx, :, :], wdst.rearrange("p a b -> p (a b)"),
                                  reads=["wbf%d_%d" % (us, q)], writes=["wind%d" % sidx])
                        else:
                            P.dma("sp", "wd%d_%d" % (us, q), wdst.rearrange("p a b -> p (a b)"), wind[sidx, :, :],
                                  reads=["wind%d" % sidx], writes=["wbf%d_%d" % (us, q)])
                        yield

            def g_load_weights(unit, hold):
                idx = wq_index[unit]
                yield from g_issue_upto(idx)
                hold[0] = wq_slot[idx]
                yield from g_issue_upto(idx + 1)

            def run(gen):
                for _ in gen:
                    pass

            def pump(gen, n):
                if gen is None:
                    return
                for _ in range(n):
                    try:
                        next(gen)
                    except StopIteration:
                        return

            def merge(ga, gb):
                alive = [ga, gb]
                while alive:
                    for g in list(alive):
                        try:
                            next(g)
                            yield
                        except StopIteration:
                            alive.remove(g)

            def chain(*gens):
                for g in gens:
                    yield from g


            pjc = [0]

            pend_ev = [None]

            def g_inproj(us, q, evac):
                for hf in range(2):
                    b = pjc[0] % 2
                    pjc[0] += 1
                    off = hf * HALF
                    for kc in range(16):
                        P.op("pe", lambda e, kc=kc, b=b, off=off: e.matmul(pj[b][:, 0:HALF], lhsT=wbf[us][:, q, kc, :],
                                                                          rhs=hnT[:, kc, off:off + HALF],
                                                                          start=(kc == 0), stop=(kc == 15)),
                             ["wbf%d_%d" % (us, q), "hnT"], ["B%d" % b])
                        if kc % 4 == 3:
                            yield
                    if pend_ev[0] is not None:
                        pend_ev[0]()
                    pend_ev[0] = (lambda evac=evac, b=b, off=off: evac(pj[b][:, 0:HALF], off, "B%d" % b))
                    yield

            def flush_ev():
                if pend_ev[0] is not None:
                    pend_ev[0]()
                    pend_ev[0] = None

            def inproj(us, q, evac):
                run(g_inproj(us, q, evac))
                flush_ev()

            xcnt = [0]
            def g_rwkv_inproj(hp, blk):
                pb = 16 + 12 * hp

                def pcol(i, pb=pb):
                    return pv[:, pb + i:pb + i + 1]
                hold = [0]
                yield from g_load_weights(("hp", hp, blk), hold)
                us = hold[0]
                pr = Praw[hp % 2]
                prq = ["Praw%d_%d" % (hp % 2, q) for q in range(4)]
                P.op("act", lambda e: e.copy(out=pr[:, :, 0], in_=rcarry[:, hp, :]), ["rcarry"], prq)
                for q in range(4):
                    def ev(psap, off, pres, q=q):
                        P.op("act", lambda e: e.copy(out=pr[:, q, 1 + off:1 + off + HALF], in_=psap), [pres], [prq[q]])
                    yield from g_inproj(us, q, ev)
                flush_ev()
                P.op("act", lambda e: e.copy(out=rcarry[:, hp, :], in_=pr[:, :, TB]), prq, ["rcarry"])

            pend_tr = [None]

            def g_flush_tr():
                if pend_tr[0] is not None:
                    t = pend_tr[0]
                    pend_tr[0] = None
                    yield from t()

            def g_prep(hp, hl, bs):
                pb = 16 + 12 * hp

                def pcol(i, pb=pb):
                    return pv[:, pb + i:pb + i + 1]
                pr = Praw[hp % 2]
                prq = ["Praw%d_%d" % (hp % 2, q) for q in range(4)]
                QRh, KBh, KVth, gamh = QR[bs][hl], KB[bs][hl], KVt[bs][hl], gam[bs][hl]
                qrn, kbn, kvn, gmn = "QR%d_%d" % (bs, hl), "KB%d_%d" % (bs, hl), "KVt%d_%d" % (bs, hl), "gam%d_%d" % (bs, hl)
                bon, bonn = bonus[bs][hl], "bonus%d_%d" % (bs, hl)
                r_, k_, v_, g_ = pr[:, 0, 0:TB], pr[:, 1, 0:TB], pr[:, 2, 0:TB], pr[:, 3, 0:TB]
                sg, a_, kk, tk, k2, b_, csg, csx = tmp[1], tmp[2], tmp[3], tmp[4], tmp[5], tmp[6], tmp[7], tmp[4]
                e1, e2, e3 = tmp[0], tmp[4], tmp[8]
                kt, bt = tmp[1], tmp[3]

                def v3(t):
                    return t[:].rearrange("p (c s) -> p c s", s=64)

                def v3a(ap):
                    return ap.rearrange("p (c s) -> p c s", s=64)

                def shift(q):
                    P.op("dve", lambda e: e.tensor_sub(out=tmp[8][:], in0=pr[:, q, 0:TB], in1=pr[:, q, 1:TB + 1]), [prq[q]], ["tmp8"])
                    P.op("dve", lambda e: e.scalar_tensor_tensor(out=pr[:, q, 0:TB], in0=tmp[8][:], scalar=pcol(q),
                                                                 in1=pr[:, q, 1:TB + 1], op0=ALU.mult, op1=ALU.add),
                         ["tmp8", prq[q], "pv"], [prq[q]])

                def lora_mm(hf):
                    off = hf * HALF
                    P.op("pe", lambda e: e.matmul(pj[0][:, 0:HALF], lhsT=loraW[:, hp * 128:(hp + 1) * 128],
                                                  rhs=lwin[:, off:off + HALF], start=True, stop=True), ["loraW", "lwin"], ["B0"])
                    P.op("pe", lambda e: e.matmul(pj[1][:, 0:HALF], lhsT=loraA[:, hp * 128:(hp + 1) * 128],
                                                  rhs=lwin[:, off:off + HALF], start=True, stop=True), ["loraA", "lwin"], ["B1"])

                def lora_ev(hf):
                    off = hf * HALF
                    P.op("act", lambda e: e.activation(out=sg[:, off:off + HALF], in_=pj[0][:, 0:HALF], func=AF.Sigmoid,
                                                       bias=pcol(4)), ["B0", "pv"], ["tmp1"])
                    P.op("act", lambda e: e.activation(out=a_[:, off:off + HALF], in_=pj[1][:, 0:HALF], func=AF.Sigmoid,
                                                       bias=pcol(5)), ["B1", "pv"], ["tmp2"])

                def stat_mm(src, srcn):
                    for hf in range(2):
                        off = hf * HALF
                        P.op("pe", lambda e, off=off, hf=hf: e.matmul(pj[hf][:, 0:HALF], lhsT=onesbd, rhs=src[:, off:off + HALF],
                                                                      start=True, stop=True), ["cs", srcn], ["B%d" % hf])

                lora_mm(0)
                yield
                yield
                shift(1)
                yield
                yield
                P.op("act", lambda e: e.activation(out=kk[:], in_=k_, func=AF.Identity, scale=pcol(6)), [prq[1], "pv"], ["tmp3"])
                P.op("act", lambda e: e.activation(out=tmp[0][:], in_=k_, func=AF.Square, scale=pcol(6)), [prq[1], "pv"], ["tmp0"])
                yield
                yield
                lora_ev(0)
                yield
                yield
                shift(0)
                yield
                yield
                shift(2)
                yield
                yield
                lora_mm(1)
                yield
                yield
                shift(3)
                yield
                yield
                lora_ev(1)
                yield
                yield
                P.op("act", lambda e: e.activation(out=sgate[bs][hl][:], in_=g_, func=AF.Silu), [prq[3]], ["sgate%d_%d" % (bs, hl)])
                yield
                yield
                P.op("dve", lambda e: e.tensor_tensor_scan(out=csg[:], data0=scanm, data1=sg[:], initial=0.0, op0=ALU.mult,
                                                           op1=ALU.add), ["cs", "tmp1"], ["tmp7"])
                yield
                yield
                stat_mm(tmp[0], "tmp0")
                yield
                yield
                P.op("act", lambda e: e.activation(out=tk[:], in_=a_[:], func=AF.Identity, scale=pcol(7), bias=pv[:, pb + 11:pb + 12]),
                     ["tmp2", "pv"], ["tmp4"])
                P.op("dve", lambda e: e.tensor_mul(out=k2[:], in0=k_, in1=tk[:]), [prq[1], "tmp4"], ["tmp5"])
                P.op("dve", lambda e: e.scalar_tensor_tensor(out=tmp[6][:], in0=r_, scalar=pcol(8), in1=k2[:], op0=ALU.mult,
                                                             op1=ALU.mult), [prq[0], "pv", "tmp5"], ["tmp6"])
                P.op("dve", lambda e: e.tensor_sub(out=csx[:], in0=csg[:], in1=sg[:]), ["tmp7", "tmp1", "tmp5"], ["tmp4"])
                yield
                yield
                P.op("act", lambda e: e.activation(out=e3[:], in_=csg[:], func=AF.Exp, scale=C0), ["tmp7"], ["tmp8"])
                P.op("act", lambda e: e.activation(out=gamh[:], in_=csg[:].rearrange("p (c s) -> p c s", s=64)[:, :, 63],
                                                   func=AF.Exp, scale=-C0), ["tmp7"], [gmn])
                yield
                yield
                P.op("dve", lambda e: e.tensor_mul(out=kt[:], in0=k2[:], in1=e3[:]), ["tmp5", "tmp8", "tmp1", "tmp7"], ["tmp1"])
                yield
                yield
                for hf in range(2):
                    off = hf * HALF
                    P.op("dve", lambda e, off=off, hf=hf: e.tensor_scalar(out=tmp[0][:, off:off + HALF], in0=pj[hf][:, 0:HALF],
                                                                          scalar1=1e-24, scalar2=None, op0=ALU.max),
                         ["B%d" % hf], ["tmp0"])
                yield
                yield
                yield from g_flush_tr()
                stat_mm(tmp[6], "tmp6")
                yield
                yield
                P.op("act", lambda e: e.activation(out=tmp[0][:], in_=tmp[0][:], func=AF.Sqrt), ["tmp0"], ["tmp0"])
                yield
                yield
                P.op("act", lambda e: e.activation(out=e2[:], in_=csx[:], func=AF.Exp, scale=-C0), ["tmp4"], ["tmp4"])
                yield
                yield
                P.op("dve", lambda e: e.tensor_tensor(out=KBH[:, :, 0, :], in0=v3(kt), in1=gamh[:, :].unsqueeze(2).to_broadcast([128, NCH, 64]),
                                                      op=ALU.mult), ["tmp1", gmn], ["KBH"])
                yield
                yield
                P.op("dve", lambda e: e.reciprocal(out=tmp[0][:], in_=tmp[0][:]), ["tmp0"], ["tmp0"])
                P.op("dve", lambda e: e.tensor_mul(out=kk[:], in0=kk[:], in1=tmp[0][:]), ["tmp3", "tmp0"], ["tmp3"])
                yield
                yield
                P.op("dve", lambda e: e.tensor_mul(out=b_[:], in0=kk[:], in1=a_[:]), ["tmp3", "tmp2"], ["tmp6"])
                yield
                yield
                for hf in range(2):
                    off = hf * HALF
                    P.op("dve", lambda e, off=off, hf=hf: e.tensor_mul(out=bon[:, off:off + HALF], in0=pj[hf][:, 0:HALF],
                                                                       in1=pr[:, 2, off:off + HALF]), ["B%d" % hf, prq[2]], [bonn])
                yield
                yield
                P.op("act", lambda e: e.copy(out=KBh[:, :, 0, :], in_=v3(kt)), ["tmp1"], [kbn])
                P.op("act", lambda e: e.copy(out=Vb[:], in_=v_), [prq[2]], ["Vb"])
                yield
                yield
                P.op("dve", lambda e: e.tensor_mul(out=QRh[:, :, 0, :], in0=v3(kk), in1=v3(e2)), ["tmp3", "tmp4"], [qrn])
                yield
                yield
                P.op("dve", lambda e: e.tensor_mul(out=bt[:], in0=b_[:], in1=e3[:]), ["tmp6", "tmp8", qrn], ["tmp3"])
                yield
                yield
                e1 = tmp[6]
                P.op("act", lambda e: e.activation(out=e1[:], in_=csg[:], func=AF.Exp, scale=-C0), ["tmp7", "tmp3"], ["tmp6"])
                yield
                yield
                gam_b = gamh[:, :].unsqueeze(2).to_broadcast([128, NCH, 64])
                P.op("act", lambda e: e.copy(out=KBh[:, :, 1, :], in_=v3(bt)), ["tmp3", kbn], [kbn])
                P.op("dve", lambda e: e.tensor_tensor(out=KBH[:, :, 1, :], in0=v3(bt), in1=gam_b, op=ALU.mult),
                     ["tmp3", gmn, "KBH"], ["KBH"])
                yield
                yield
                P.op("dve", lambda e: e.tensor_mul(out=QRh[:, :, 1, :], in0=v3a(r_), in1=v3(e1)), [prq[0], "tmp6", qrn], [qrn])
                yield
                yield
                def tr():
                    for c2 in range(0, NCH, 2):
                        for cc in range(2):
                            c = c2 + cc
                            for h in range(2):
                                ph = slice(64 * h, 64 * h + 64)
                                for m in range(3):
                                    src = KBH[ph, c, m, :] if m < 2 else Vb[ph, c * 64:(c + 1) * 64]
                                    P.op("pe", lambda e, src=src, ph=ph, h=h, cc=cc, m=m: e.matmul(
                                        pj[c2 // 2 % 2][ph, cc * 192 + m * 64:cc * 192 + m * 64 + 64], lhsT=src,
                                        rhs=identb[ph, 64 * h:64 * h + 64], start=True, stop=True, tile_position=(64 * h, 64 * h)),
                                        ["KBH", "Vb", "identb"], ["B%d" % (c2 // 2 % 2)])
                            yield
                            yield
                        if c2 >= 2:
                            pc2 = c2 - 2
                            P.op("act", lambda e, pc2=pc2: e.copy(out=KVth[:, pc2:pc2 + 2, :, :].rearrange("p a b c -> p (a b c)"),
                                                                  in_=pj[pc2 // 2 % 2][:, 0:384]), ["B%d" % (pc2 // 2 % 2)], [kvn])
                    pc2 = NCH - 2
                    P.op("act", lambda e: e.copy(out=KVth[:, pc2:pc2 + 2, :, :].rearrange("p a b c -> p (a b c)"),
                                                 in_=pj[pc2 // 2 % 2][:, 0:384]), ["B%d" % (pc2 // 2 % 2)], [kvn])
                pend_tr[0] = tr

            def stage2(grp, bs, filler, rate, part):
                def heads():
                    for h in range(2):
                        yield slice(64 * h, 64 * h + 64), (64 * h, 64 * h), h

                def stA(c, hl):
                    am = AM[hl][c % 3]
                    amn = "AM%d_%d" % (hl, c % 3)
                    pah, pan = pa[hl], PAN[hl]
                    QRh, KBh = QR[bs][hl], KB[bs][hl]
                    qrn, kbn = "QR%d_%d" % (bs, hl), "KB%d_%d" % (bs, hl)
                    for ph, tp, h in heads():
                        P.op("pe", lambda e, ph=ph, tp=tp: e.matmul(pah[ph, 0:128], lhsT=KBh[ph, c, 0, :],
                                                                    rhs=QRh[ph, c, :, :].rearrange("p a b -> p (a b)"),
                                                                    start=True, stop=True, tile_position=tp), [kbn, qrn], [pan])
                        P.op("pe", lambda e, ph=ph, tp=tp: e.matmul(pah[ph, 128:256], lhsT=KBh[ph, c, 1, :],
                                                                    rhs=QRh[ph, c, :, :].rearrange("p a b -> p (a b)"),
                                                                    start=True, stop=True, tile_position=tp), [kbn, qrn], [pan])
                        P.op("pe", lambda e, ph=ph, tp=tp: e.matmul(pah[ph, 256:320], lhsT=QRh[ph, c, 0, :], rhs=KBh[ph, c, 1, :],
                                                                    start=True, stop=True, tile_position=tp), [kbn, qrn], [pan])
                    P.op("dve", lambda e: e.tensor_tensor(out=am[:].rearrange("p a b -> p (a b)"), in0=pah[:, 0:256], in1=mask4,
                                                          op=ALU.mult), [pan, "cs"], [amn])
                    s0 = S[hl][c % 2][0]
                    s0n = "S%d_%d_0" % (hl, c % 2)
                    P.op("act", lambda e: e.copy(out=s0[:, 0, :], in_=am[:, 2, :]), [amn], [s0n])
                    P.op("dve", lambda e: e.tensor_tensor(out=s0[:, 2, :], in0=pah[:, 256:320], in1=maskx, op=ALU.mult),
                         [pan, "cs", s0n], [s0n])

                def stInv(c, hl, kstep):
                    si, so = S[hl][c % 2][kstep % 2], S[hl][c % 2][(kstep + 1) % 2]
                    sin, son = "S%d_%d_%d" % (hl, c % 2, kstep % 2), "S%d_%d_%d" % (hl, c % 2, (kstep + 1) % 2)
                    last = kstep == 5
                    pih = pa[hl][:, 320:512] if kstep < 2 else PI[hl][:, 0:192]
                    pin = PAN[hl] if kstep < 2 else PIN[hl]
                    pk = SI if kstep == 0 else si[:, 1, :]
                    for ph, tp, h in heads():
                        pkh = SI[ph, :] if kstep == 0 else si[ph, 1, :]
                        if kstep == 0 or last:
                            if not last:
                                P.op("pe", lambda e, ph=ph, tp=tp: e.matmul(pih[ph, 0:64], lhsT=si[ph, 2, :], rhs=si[ph, 0, :],
                                                                            start=True, stop=True, tile_position=tp), [sin], [pin])
                            P.op("pe", lambda e, ph=ph, tp=tp, pkh=pkh: e.matmul(pih[ph, 64:128], lhsT=si[ph, 2, :], rhs=pkh,
                                                                                 start=True, stop=False, tile_position=tp),
                                 [sin, "SI"], [pin])
                        else:
                            P.op("pe", lambda e, ph=ph, tp=tp: e.matmul(pih[ph, 0:128], lhsT=si[ph, 2, :],
                                                                        rhs=si[ph, 0:2, :].rearrange("p a b -> p (a b)"),
                                                                        start=True, stop=False, tile_position=tp), [sin], [pin])
                        P.op("pe", lambda e, ph=ph, tp=tp, pkh=pkh, h=h: e.matmul(pih[ph, 64:128], lhsT=identb[ph, 64 * h:64 * h + 64],
                                                                                  rhs=pkh, start=False, stop=True, tile_position=tp),
                             [sin, "SI", "identb"], [pin])
                        if not last:
                            P.op("pe", lambda e, ph=ph, tp=tp: e.matmul(pih[ph, 128:192], lhsT=si[ph, 0, :], rhs=si[ph, 2, :],
                                                                        start=True, stop=True, tile_position=tp), [sin], [pin])
                    if not last:
                        P.op("act", lambda e: e.copy(out=so[:].rearrange("p a b -> p (a b)"), in_=pih[:, 0:192]), [pin], [son])
                    else:
                        P.op("act", lambda e: e.copy(out=INV[hl][c % 3][:], in_=pih[:, 64:128]), [pin], ["INV%d_%d" % (hl, c % 3)])

                def stRHS(c, hl):
                    hp = grp * 2 + hl
                    am, amn = AM[hl][c % 3], "AM%d_%d" % (hl, c % 3)
                    par = c % 2
                    t0, t0n = T0b[:, hp, par, :], "T0b%d_%d" % (hp, par)
                    pch = PC[hl][:, 0:192]
                    for ph, tp, h in heads():
                        P.op("pe", lambda e, ph=ph, tp=tp: e.matmul(pch[ph, 0:64], lhsT=am[ph, 0, :], rhs=KVt[bs][hl][ph, c, 2, :],
                                                                    start=True, stop=False, tile_position=tp),
                             [amn, "KVt%d_%d" % (bs, hl)], [PCN[hl]])
                        P.op("pe", lambda e, ph=ph, tp=tp: e.matmul(pch[ph, 0:64], lhsT=QR[bs][hl][ph, c, 0, :], rhs=t0[ph, :],
                                                                    start=False, stop=True, tile_position=tp),
                             ["QR%d_%d" % (bs, hl), t0n, "T0b"], [PCN[hl]])
                    P.op("act", lambda e: e.copy(out=RHSb[hl][:], in_=pch[:, 0:64]), [PCN[hl]], ["RHSb%d" % hl])

                def stU(c, hl):
                    pch = PC[hl][:, 0:192]
                    inv, invn = INV[hl][c % 3], "INV%d_%d" % (hl, c % 3)
                    for ph, tp, h in heads():
                        P.op("pe", lambda e, ph=ph, tp=tp: e.matmul(pch[ph, 64:128], lhsT=inv[ph, :], rhs=RHSb[hl][ph, :],
                                                                    start=True, stop=True, tile_position=tp),
                             [invn, "RHSb%d" % hl], [PCN[hl]])
                    P.op("act", lambda e: e.mul(out=Unb[hl][:], in_=pch[:, 64:128], mul=-1.0),
                         [PCN[hl]], ["Unb%d" % hl])

                def stTY(c, hl):
                    hp = grp * 2 + hl
                    am, amn = AM[hl][c % 3], "AM%d_%d" % (hl, c % 3)
                    par = c % 2
                    t0, t0n = T0b[:, hp, par, :], "T0b%d_%d" % (hp, par)
                    t1, t1n = T0b[:, hp, 1 - par, :], "T0b%d_%d" % (hp, 1 - par)
                    pch = PC[hl][:, 0:192]
                    pyh = PC[hl][:, 192:256]
                    pyn = PCN[hl]
                    kvn, unn = "KVt%d_%d" % (bs, hl), "Unb%d" % hl
                    for ph, tp, h in heads():
                        P.op("pe", lambda e, ph=ph, tp=tp: e.matmul(pyh[ph, :], lhsT=t0[ph, :], rhs=QR[bs][hl][ph, c, 1, :],
                                                                    start=True, stop=False, tile_position=tp),
                             [t0n, "QR%d_%d" % (bs, hl), "T0b"], [pyn])
                        P.op("pe", lambda e, ph=ph, tp=tp: e.matmul(pyh[ph, :], lhsT=KVt[bs][hl][ph, c, 2, :], rhs=am[ph, 1, :],
                                                                    start=False, stop=False, tile_position=tp), [kvn, amn], [pyn])
                        P.op("pe", lambda e, ph=ph, tp=tp: e.matmul(pyh[ph, :], lhsT=Unb[hl][ph, :], rhs=am[ph, 3, :],
                                                                    start=False, stop=True, tile_position=tp), [unn, amn], [pyn])
                    for ph, tp, h in heads():
                        P.op("pe", lambda e, ph=ph, tp=tp: e.matmul(pch[ph, 128:192], lhsT=KVt[bs][hl][ph, c, 0, :], rhs=KVt[bs][hl][ph, c, 2, :],
                                                                    start=True, stop=False, tile_position=tp), [kvn], [PCN[hl]])
                        P.op("pe", lambda e, ph=ph, tp=tp: e.matmul(pch[ph, 128:192], lhsT=KVt[bs][hl][ph, c, 1, :], rhs=Unb[hl][ph, :],
                                                                    start=False, stop=True, tile_position=tp), [kvn, unn], [PCN[hl]])
                    P.op("dve", lambda e: e.scalar_tensor_tensor(out=T32[:, hp, :], in0=T32[:, hp, :], scalar=gam[bs][hl][:, c:c + 1],
                                                                 in1=pch[:, 128:192], op0=ALU.mult, op1=ALU.add),
                         ["T32_%d" % hp, "T32", "gam%d_%d" % (bs, hl), PCN[hl]], ["T32_%d" % hp])
                    P.op("act", lambda e: e.copy(out=t1, in_=T32[:, hp, :]), ["T32_%d" % hp, "T0b"], [t1n])
                    P.op("act", lambda e: e.copy(out=ybuf[hl][:, c * 64:(c + 1) * 64], in_=pyh), [pyn], ["ybuf%d" % hl])

                def inv_steps(c, ks):
                    for k in ks:
                        for hl in range(2):
                            stInv(c, hl, k)
                        pump(filler, rate)

                if part == "prologue":
                    def gen():
                        for hl in range(2):
                            stA(0, hl)
                            yield
                        for k in range(6):
                            for hl in range(2):
                                stInv(0, hl, k)
                                yield
                        if NCH > 1:
                            for hl in range(2):
                                stA(1, hl)
                                yield
                            for k in (0, 1):
                                for hl in range(2):
                                    stInv(1, hl, k)
                                    yield
                    return gen()
                for c in range(NCH):
                    n1, n2 = c + 1 < NCH, c + 2 < NCH
                    if n2:
                        for hl in range(2):
                            stA(c + 2, hl)
                        pump(filler, rate)
                    for hl in range(2):
                        stRHS(c, hl)
                    pump(filler, rate)
                    if n1:
                        inv_steps(c + 1, (2,))
                    if n2:
                        inv_steps(c + 2, (0,))
                    for hl in range(2):
                        stU(c, hl)
                    pump(filler, rate)
                    if n1:
                        inv_steps(c + 1, (3,))
                    if n2:
                        inv_steps(c + 2, (1,))
                    if n1:
                        inv_steps(c + 1, (4,))
                    for hl in range(2):
                        stTY(c, hl)
                    pump(filler, rate)
                    if n1:
                        inv_steps(c + 1, (5,))
                if filler is not None:
                    run(filler)

            def stage3(grp, bs, blk, as_gen=False):
                def one(hl):
                    hp = grp * 2 + hl
                    pb = 16 + 12 * hp
                    y_ = ybuf[hl]
                    yn_ = "ybuf%d" % hl
                    ti = (4, 5, 6) if hl == 0 else (1, 2, 3)
                    ysq, m_, r2 = tmp[ti[0]], tmp[ti[1]], tmp[ti[2]]
                    nsq, nm, nr = "tmp%d" % ti[0], "tmp%d" % ti[1], "tmp%d" % ti[2]
                    pbk = (pj[0], pj[1]) if hl == 0 else (PC[0], PC[1])
                    pbn = ("B0", "B1") if hl == 0 else (PCN[0], PCN[1])
                    P.op("dve", lambda e: e.tensor_mul(out=ysq[:], in0=y_[:], in1=y_[:]), [yn_], [nsq])
                    yield
                    for hf in range(2):
                        off = hf * HALF
                        P.op("pe", lambda e, off=off: e.matmul(pbk[0][:, 0:HALF], lhsT=onesbd, rhs=y_[:, off:off + HALF],
                                                               start=True, stop=True), ["cs", yn_], [pbn[0]])
                        P.op("pe", lambda e, off=off: e.matmul(pbk[1][:, 0:HALF], lhsT=onesbd, rhs=ysq[:, off:off + HALF],
                                                               start=True, stop=True), ["cs", nsq], [pbn[1]])
                        yield
                        P.op("dve", lambda e, off=off: e.tensor_scalar(out=m_[:, off:off + HALF], in0=pbk[0][:, 0:HALF], scalar1=1.0 / 64,
                                                                       scalar2=None, op0=ALU.mult), [pbn[0]], [nm])
                        P.op("dve", lambda e, off=off: e.tensor_mul(out=r2[:, off:off + HALF], in0=m_[:, off:off + HALF],
                                                                    in1=m_[:, off:off + HALF]), [nm], [nr])
                        P.op("dve", lambda e, off=off: e.scalar_tensor_tensor(out=r2[:, off:off + HALF], in0=pbk[1][:, 0:HALF], scalar=1.0 / 64,
                                                                              in1=r2[:, off:off + HALF], op0=ALU.mult, op1=ALU.subtract),
                             [pbn[1], nr], [nr])
                        yield
                    P.op("dve", lambda e: e.tensor_scalar(out=r2[:], in0=r2[:], scalar1=64e-5, scalar2=None, op0=ALU.add), [nr], [nr])
                    yield
                    P.op("act", lambda e: e.activation(out=r2[:], in_=r2[:], func=AF.Sqrt), [nr], [nr])
                    yield
                    P.op("dve", lambda e: e.tensor_sub(out=y_[:], in0=y_[:], in1=m_[:]), [yn_, nm], [yn_])
                    yield
                    P.op("dve", lambda e: e.reciprocal(out=ysq[:], in_=r2[:]), [nr, nsq], [nsq])
                    P.op("dve", lambda e: e.tensor_mul(out=y_[:], in0=y_[:], in1=ysq[:]), [yn_, nsq], [yn_])
                    yield
                    P.op("dve", lambda e: e.tensor_scalar(out=y_[:], in0=y_[:], scalar1=pv[:, pb + 9:pb + 10],
                                                          scalar2=pv[:, pb + 10:pb + 11], op0=ALU.mult, op1=ALU.add), [yn_, "pv"], [yn_])
                    P.op("dve", lambda e: e.tensor_add(out=y_[:], in0=y_[:], in1=bonus[bs][hl][:]), [yn_, "bonus%d_%d" % (bs, hl)], [yn_])
                    yield
                    mx = mixT[hp % 2]
                    mxn = "mixT%d" % (hp % 2)
                    P.op("dve", lambda e: e.tensor_mul(out=mx[:], in0=y_[:], in1=sgate[bs][hl][:]), [yn_, "sgate%d_%d" % (bs, hl)], [mxn])
                    P.dma("pool", "m%d" % (hp % 2), mixd[blk][hp, :, :], mx[:], reads=[mxn], writes=["mixd%d_%d" % (blk, hp)])
                g3 = merge(one(0), one(1))
                if as_gen:
                    return g3
                run(g3)

            def g_conv_unit(cg, blk):
                cb = 16 + 12 * NHP + 1 + 3 * cg
                cbase = NHP * 512 + 128
                hold = [0]
                yield from g_load_weights(("cv", cg, blk), hold)
                us = hold[0]
                pr = Praw[cg % 2]
                prq = ["Praw%d_%d" % (cg % 2, q) for q in range(4)]
                for q in range(4):
                    def ev(psap, off, pres, q=q):
                        P.op("act", lambda e: e.copy(out=pr[:, q, 1 + off:1 + off + HALF], in_=psap), [pres], [prq[q]])
                    yield from g_inproj(us, q, ev)
                flush_ev()
                yield
                P.op("act", lambda e: e.copy(out=Ubuf[:, 0:2], in_=ccarry[:, cg, :]), ["ccarry"], ["tmp7"])
                P.op("dve", lambda e: e.tensor_mul(out=Ubuf[:, 2:TB + 2], in0=pr[:, 1, 1:TB + 1], in1=pr[:, 2, 1:TB + 1]),
                     [prq[1], prq[2], "tmp7"], ["tmp7"])
                P.op("act", lambda e: e.copy(out=ccarry[:, cg, :], in_=Ubuf[:, TB:TB + 2]), ["tmp7"], ["ccarry"])
                cv = tmp[0]
                yield
                P.op("dve", lambda e: e.tensor_scalar(out=cv[:], in0=Ubuf[:, 0:TB], scalar1=pv[:, cb:cb + 1], scalar2=None,
                                                      op0=ALU.mult), ["tmp7", "pv"], ["tmp0"])
                P.op("dve", lambda e: e.scalar_tensor_tensor(out=cv[:], in0=Ubuf[:, 1:TB + 1], scalar=pv[:, cb + 1:cb + 2], in1=cv[:],
                                                             op0=ALU.mult, op1=ALU.add), ["tmp7", "pv", "tmp0"], ["tmp0"])
                P.op("dve", lambda e: e.scalar_tensor_tensor(out=cv[:], in0=Ubuf[:, 2:TB + 2], scalar=pv[:, cb + 2:cb + 3], in1=cv[:],
                                                             op0=ALU.mult, op1=ALU.add), ["tmp7", "pv", "tmp0"], ["tmp0"])
                P.op("dve", lambda e: e.tensor_mul(out=cv[:], in0=cv[:], in1=pr[:, 0, 1:TB + 1]), ["tmp0", prq[0]], ["tmp0"])
                yield
                P.op("act", lambda e: e.activation(out=tmp[5][:], in_=pr[:, 3, 1:TB + 1], func=AF.Silu), [prq[3]], ["tmp5"])
                u = NHP + cg
                mx = mixT[u % 2]
                mxn = "mixT%d" % (u % 2)
                P.op("dve", lambda e: e.tensor_mul(out=mx[:], in0=cv[:], in1=tmp[5][:]), ["tmp0", "tmp5"], [mxn])
                P.dma("pool", "m%d" % (u % 2), mixd[blk][u, :, :], mx[:], reads=[mxn], writes=["mixd%d_%d" % (blk, u)])


            def g_xphase(blk):
                pend = [None]

                def transposes(sl, tcols):
                    for g4 in range(4):
                        for j in range(4):
                            kc = g4 * 4 + j
                            P.op("pe", lambda e, kc=kc, j=j: e.matmul(pj[g4 % 2][:, j * 128:(j + 1) * 128],
                                                                      lhsT=xs[sl][:, kc * 128:(kc + 1) * 128],
                                                                      rhs=identb[:, :], start=True, stop=True),
                                 ["xs%d" % sl, "identb"], ["B%d" % (g4 % 2)])
                        yield
                        if g4 % 2 == 0:
                            P.op("act", lambda e, g4=g4: e.copy(out=hnT[:, g4 * 4:g4 * 4 + 4, tcols],
                                                                in_=pj[g4 % 2][:, :].rearrange("p (a b) -> p a b", b=128)),
                                 ["B%d" % (g4 % 2)], ["hnT"])
                        else:
                            P.op("dve", lambda e, g4=g4: e.tensor_copy(out=hnT[:, g4 * 4:g4 * 4 + 4, tcols],
                                                                       in_=pj[g4 % 2][:, :].rearrange("p (a b) -> p a b", b=128)),
                                 ["B%d" % (g4 % 2)], ["hnT"])
                        yield
                for ti in range(TPB):
                    gt = blk * TPB + ti
                    tcols = slice(ti * 128, (ti + 1) * 128)
                    if gt < NZ:
                        P.op("pool", lambda e, tcols=tcols: e.memset(hnT[:, :, tcols], 0.0), [], ["hnT"])
                        continue
                    sl = xcnt[0] % 2
                    xcnt[0] += 1
                    if gt == NZ:
                        P.op("pool", lambda e: e.memset(xt[0][:], 0.0), [], ["xt0"])
                        P.dma("sp", "x0", xt[0][112:128, :], meta[:, :], reads=["xt0"], writes=["xt0"])
                    else:
                        r0 = (gt - NZ - 1) * 128
                        P.dma("sp", "x0", xt[0][:], x[r0:r0 + 128, :], writes=["xt0"])
                    yield
                    P.op("pool", lambda e: e.memset(ssq[:, 0:1], 0.0), [], ["ssq"])
                    P.op("act", lambda e, sl=sl: e.activation(out=xs[sl][:], in_=xt[0][:], func=AF.Square,
                                                              accum_out=ssq[:, 0:1]), ["xt0", "ssq"], ["xs%d" % sl, "ssq", "lst"])
                    yield
                    P.op("dve", lambda e: e.tensor_scalar(out=ssq[:, 2:3], in0=ssq[:, 0:1], scalar1=1.0 / D, scalar2=1e-6,
                                                          op0=ALU.mult, op1=ALU.add), ["ssq"], ["ssq"])
                    yield
                    P.op("act", lambda e: e.activation(out=ssq[:, 1:2], in_=ssq[:, 2:3], func=AF.Sqrt), ["ssq"], ["ssq"])
                    yield
                    P.op("dve", lambda e: e.reciprocal(out=ssq[:, 3:4], in_=ssq[:, 1:2]), ["ssq"], ["ssq"])
                    P.op("dve", lambda e, sl=sl: e.tensor_scalar(out=xs[sl][:], in0=xt[0][:], scalar1=ssq[:, 3:4], scalar2=None,
                                                                 op0=ALU.mult), ["xt0", "ssq"], ["xs%d" % sl])
                    yield
                    if pend[0] is not None:
                        yield from transposes(*pend[0])
                    pend[0] = (sl, tcols)
                if pend[0] is not None:
                    yield from transposes(*pend[0])


            def g_lora(blk):
                hold = [0]
                yield from g_load_weights(("lora", 0, blk), hold)
                us = hold[0]
                yield
                P.op("act", lambda e: e.copy(out=Plo[:, 0:1], in_=lcarry[:, 0:1]), ["lcarry"], ["Plo"])

                def ev_lora(psap, off, pres):
                    P.op("act", lambda e: e.copy(out=Plo[:, 1 + off:1 + off + HALF], in_=psap), [pres], ["Plo"])
                yield from g_inproj(us, 0, ev_lora)
                flush_ev()
                yield
                P.op("act", lambda e: e.copy(out=lcarry[:, 0:1], in_=Plo[:, TB:TB + 1]), ["Plo"], ["lcarry"])
                mucol = pv[:, 16 + 12 * NHP:16 + 12 * NHP + 1]
                yield
                P.op("dve", lambda e: e.tensor_sub(out=tmp[0][:], in0=Plo[:, 0:TB], in1=Plo[:, 1:TB + 1]), ["Plo"], ["tmp0"])
                yield
                P.op("dve", lambda e: e.scalar_tensor_tensor(out=tmp[1][:], in0=tmp[0][:], scalar=mucol, in1=Plo[:, 1:TB + 1],
                                                             op0=ALU.mult, op1=ALU.add), ["tmp0", "Plo", "pv"], ["tmp1"])
                yield
                P.op("act", lambda e: e.activation(out=lwin[0:64, :], in_=tmp[1][0:64, :], func=AF.Tanh), ["tmp1"], ["lwin"])
                yield
                P.op("dve", lambda e: e.tensor_copy(out=lwin[64:128, :], in_=tmp[1][64:128, :]), ["tmp1", "lwin"], ["lwin"])


            def block_tail(blk):
                if blk > 1:
                    gc_copy(blk - 2)
                if 1 <= blk <= 4:
                    for u in range((blk - 1) * 4, blk * 4):
                        wout_cast(u)
                if blk > 0:
                    P.collective(mixd[blk - 1].rearrange("u p t -> (u p) t"), gath[blk - 1].rearrange("u p t -> (u p) t"), PAIRS,
                                 reads=["mixd%d_%d" % (blk - 1, u) for u in range(NU)], writes=["gath%d" % (blk - 1)])

            run(g_xphase(0))
            run(g_lora(0))
            run(g_rwkv_inproj(0, 0))
            run(g_rwkv_inproj(1, 0))
            run(g_prep(0, 0, 0))
            run(g_prep(1, 1, 0))
            run(g_flush_tr())
            run(stage2(0, 0, None, 0, "prologue"))
            for blk in range(NBLK):
                block_tail(blk)
                f0 = chain(g_rwkv_inproj(2, blk), g_rwkv_inproj(3, blk), g_prep(2, 0, 1), g_prep(3, 1, 1), g_conv_unit(0, blk), g_flush_tr())
                stage2(0, 0, f0, 2, "main")
                run(merge(stage2(1, 1, None, 0, "prologue"), stage3(0, 0, blk, as_gen=True)))
                gens = [g_conv_unit(cg, blk) for cg in range(1, NCG)]
                if blk + 1 < NBLK:
                    gens += [g_xphase(blk + 1), g_lora(blk + 1), g_rwkv_inproj(0, blk + 1), g_rwkv_inproj(1, blk + 1),
                             g_prep(0, 0, 0), g_prep(1, 1, 0), g_flush_tr()]
                stage2(1, 1, chain(*gens), 3, "main")
                if blk + 1 < NBLK:
                    run(merge(stage2(0, 0, None, 0, "prologue"), stage3(1, 1, blk, as_gen=True)))
                else:
                    stage3(1, 1, blk)

        with ExitStack() as s2:
            def sb2(name, shape, dt=F32):
                return s2.enter_context(nc.sbuf_tensor(name, shape, dt))
            woutb = sb2("woutb", [128, NG, D], BF16)
            nf = sb2("nf", [128, D])
            mixl = [sb2("mixl%d" % i, [128, NG, 128], BF16) for i in range(2)]
            xr = [sb2("xr%d" % i, [128, D]) for i in range(2)]
            hh = [sb2("hh%d" % i, [128, D]) for i in range(2)]
            oo = [sb2("oo%d" % i, [128, D]) for i in range(2)]
            junk2 = sb2("junk2", [128, D], BF16)
            sq2 = [sb2("sq2_%d" % i, [128, 16]) for i in range(2)]
            bar = sb2("bar", [128, 16])
            dram_pref = ("gath", "mixd", "wind", "woutd")
            allres = [r for r in P.res.keys() if not r.startswith(dram_pref)]
            P.collective(mixd[NBLK - 1].rearrange("u p t -> (u p) t"), gath[NBLK - 1].rearrange("u p t -> (u p) t"), PAIRS,
                         reads=["mixd%d_%d" % (NBLK - 1, u) for u in range(NU)], writes=["gath%d" % (NBLK - 1)])
            P.dma("sp", "c3", bar[:, 0:8], normf[0:128, 0:8], reads=allres, writes=["nf"] + allres)
            def wout_chunk(oc):
                for u in range(NG):
                    P.dma("sp", "wo%d" % oc, woutb[:, u, oc * 512:(oc + 1) * 512], woutd[u * 128:(u + 1) * 128, oc * 512:(oc + 1) * 512],
                          reads=["woutd%d" % u, "nf"], writes=["woutb%d" % oc] if u == NG - 1 else [])
            wout_chunk(0)
            par2 = nc.sync.partition_id() % 2
            NT2 = 16
            pbank = [PSB[0], PSB[1], PSB[3], PSB[4]]
            pbn = ["B0", "B1", "B3", "B4"]
            pjc2 = 0
            pend = None
            for i in range(NT2):
                sl = i % 2
                if i == 6:
                    gc_copy(NBLK - 1)
                tile0 = par2 * NT2 + i
                gtmax = NZ + 1 + NT2 + i
                P.dma("sp", "ml%d" % sl, mixl[sl][:], gathall[:, bass.ds(par2 * (NT2 * NG * 128) + (NZ + 1 + i) * NG * 128, NG * 128)].rearrange("p (u t) -> p u t", t=128),
                      reads=["nf", "gathall%d_%d" % (gtmax // TPB, gtmax % TPB), "gathall%d_%d" % ((gtmax - NT2) // TPB, (gtmax - NT2) % TPB)],
                      writes=["mixl%d" % sl])
                P.dma("sp", "xr%d" % sl, xr[sl][:], xres[i * 128:(i + 1) * 128, :], reads=["nf"], writes=["xr%d" % sl])
                if i == 0:
                    for oc in range(1, 4):
                        wout_chunk(oc)
                    P.dma("sp", "c5", nf[:], normf[:, :], reads=["nf"], writes=["nf"])
                    gc_copy(NBLK - 2)
                hs = hh[sl]
                for oc in range(4):
                    b = pjc2 % 4
                    pjc2 += 1
                    for u in range(NG):
                        P.op("pe", lambda e, u=u, sl=sl, oc=oc, b=b: e.matmul(pbank[b][:, :], lhsT=mixl[sl][:, u, :],
                                                                             rhs=woutb[:, u, oc * 512:(oc + 1) * 512],
                                                                             start=(u == 0), stop=(u == NG - 1)),
                             ["mixl%d" % sl, "woutb%d" % oc], [pbn[b]])
                    P.op("dve", lambda e, sl=sl, oc=oc, b=b, hs=hs: e.tensor_tensor(out=hs[:, oc * 512:(oc + 1) * 512], in0=pbank[b][:, :],
                                                                                   in1=xr[sl][:, oc * 512:(oc + 1) * 512], op=ALU.add),
                         [pbn[b], "xr%d" % sl], ["hh%d" % sl])
                sq = sq2[sl][:, 0:4]
                sqn = "sq2_%d" % sl
                P.op("pool", lambda e, sq=sq: e.memset(sq[:, 0:1], 0.0), [], [sqn])
                P.op("act", lambda e, sq=sq, hs=hs: e.activation(out=junk2[:], in_=hs[:], func=AF.Square, accum_out=sq[:, 0:1]),
                     ["hh%d" % sl, sqn], ["junk2", sqn])
                if pend is not None:
                    pend()
                P.op("dve", lambda e, sq=sq: e.tensor_scalar(out=sq[:, 1:2], in0=sq[:, 0:1], scalar1=1.0 / D, scalar2=1e-6, op0=ALU.mult,
                                                             op1=ALU.add), [sqn], [sqn])
                P.op("act", lambda e, sq=sq: e.activation(out=sq[:, 2:3], in_=sq[:, 1:2], func=AF.Sqrt), [sqn], [sqn])

                def fin(i=i, sl=sl, sq=sq, hs=hs, sqn=sqn):
                    P.op("dve", lambda e: e.reciprocal(out=sq[:, 3:4], in_=sq[:, 2:3]), [sqn], [sqn])
                    P.op("dve", lambda e: e.scalar_tensor_tensor(out=oo[sl][:], in0=hs[:], scalar=sq[:, 3:4], in1=nf[:],
                                                                 op0=ALU.mult, op1=ALU.mult), ["hh%d" % sl, sqn, "nf"], ["oo%d" % sl])
                    P.dma("pool", "o%d" % sl, out[i * 128:(i + 1) * 128, :], oo[sl][:], reads=["oo%d" % sl], writes=[])
                pend = fin
            pend()
            P.finish("sp")
            P.finish("pool")
    return nc


def _consts():
    c = np.zeros((128, NCST), np.float32)
    s = np.arange(64)[:, None]
    t = np.arange(64)[None, :]
    strict = (s < t).astype(np.float32)
    incl = (s <= t).astype(np.float32)
    m4 = np.concatenate([strict, incl, -strict, incl], axis=1)
    c[0:64, CS_M4:CS_M4 + 256] = m4
    c[64:128, CS_M4:CS_M4 + 256] = m4
    mx = -(s > t).astype(np.float32)
    c[0:64, CS_MX:CS_MX + 64] = mx
    c[64:128, CS_MX:CS_MX + 64] = mx
    c[0:64, CS_ON:CS_ON + 64] = 1.0
    c[64:128, CS_ON + 64:CS_ON + 128] = 1.0
    sc = np.ones((TB,), np.float32)
    sc[::64] = 0.0
    c[:, CS_SC:CS_SC + TB] = sc[None, :]
    return c


def _colT(v):
    return np.ascontiguousarray(np.asarray(v, np.float32).reshape(-1, 128).T)


def kernel(x, meta_tokens, norm_in_g, w_in, mu_shift, w0, w_lora_up, a0, a_lora_up, k_k, k_a, r_k,
           lnx_g, lnx_b, conv_w, w_out, norm_f_g):
    x = np.asarray(x, np.float32)
    w_in0 = np.asarray(w_in, np.float32)[0]
    mu = np.asarray(mu_shift, np.float32)[0]
    cols = {
        0: _colT(mu[0:1024]), 1: _colT(mu[1024:2048]), 2: _colT(mu[2048:3072]), 3: _colT(mu[3072:4096]),
        4: _colT(np.asarray(w0)[0]), 5: _colT(np.asarray(a0)[0]), 6: _colT(np.asarray(k_k)[0]), 7: _colT(np.asarray(k_a)[0]),
        8: _colT(np.asarray(r_k)[0].reshape(-1)), 9: _colT(np.asarray(lnx_g)[0]), 10: _colT(np.asarray(lnx_b)[0]),
    }
    cw = np.asarray(conv_w, np.float32)[0]
    gin = _colT(np.asarray(norm_in_g, np.float32)[0])
    lora_full = np.concatenate([np.asarray(w_lora_up, np.float32)[0], np.asarray(a_lora_up, np.float32)[0]], axis=0)
    cst = _consts()
    ident = np.eye(128, dtype=np.float32)
    wout0 = np.asarray(w_out, np.float32)[0]
    rows = []
    for r in range(2):
        for u in range(NHP):
            rows.append(np.arange((r * NHP + u) * 128, (r * NHP + u + 1) * 128))
        for u in range(NCG):
            rows.append(1024 + np.arange((r * NCG + u) * 128, (r * NCG + u + 1) * 128))
    wout_r = np.ascontiguousarray(wout0[np.concatenate(rows)])
    normf = np.ascontiguousarray(np.broadcast_to(np.asarray(norm_f_g, np.float32)[None, :], (128, D)))
    meta = np.ascontiguousarray(np.asarray(meta_tokens, np.float32))
    shards = []
    for half in range(2):
        pvec = np.zeros((128, NPV), np.float32)
        pvec[:, 0:16] = gin
        for hp in range(NHP):
            for i, arr in cols.items():
                pvec[:, 16 + 12 * hp + i] = arr[:, half * NHP + hp]
        pvec[:, 16 + 12 * NHP] = mu[4096:4224]
        for cg in range(NCG):
            for j in range(3):
                ch0 = (half * NCG + cg) * 128
                pvec[:, 16 + 12 * NHP + 1 + 3 * cg + j] = cw[j, ch0:ch0 + 128]
        csel = []
        for q in range(4):
            csel.append(np.arange(q * 1024 + half * NHP * 128, q * 1024 + (half + 1) * NHP * 128))
        csel.append(np.arange(4096, 4224))
        for q in range(4):
            csel.append(4224 + np.arange(q * 1024 + half * NCG * 128, q * 1024 + (half + 1) * NCG * 128))
        w_sh = np.ascontiguousarray(w_in0[:, np.concatenate(csel)])
        lora = np.ascontiguousarray(lora_full[:, half * NHP * 128:(half + 1) * NHP * 128])
        shards.append((pvec, w_sh, lora))
    nc = build_nc()
    in_maps = []
    for c in range(8):
        b, half = c // 2, c % 2
        pvec, w_sh, lora = shards[half]
        in_maps.append({"x": np.ascontiguousarray(x[b]), "xres": np.ascontiguousarray(x[b, half * (SEQ // 2):(half + 1) * (SEQ // 2)]), "meta": meta, "w_in": w_sh, "pvec": pvec,
                        "cst": cst, "ident": ident, "lora": lora, "w_out": wout_r, "normf": normf})
    res = run_bass_kernel_spmd(nc, in_maps, core_ids=list(range(8)))
    return np.stack([np.concatenate([np.asarray(res.results[2 * b]["out"], np.float32),
                                     np.asarray(res.results[2 * b + 1]["out"], np.float32)], axis=0) for b in range(4)], axis=0)
```
